# Optimizing a Trainium2 kernel written in Bass

```python
import math
import jax, jax.numpy as jnp
from jax import lax
import numpy as np

D_MODEL = 1024
BATCH = 4
SEQ = 8192
DEPTH = 2
DEC_BATCH = 32
DEC_SEQ = 8
PAST_LEN = 16384
PAGE_SIZE = 128

HEAD_DIM = 64
H_A = D_MODEL // (2 * HEAD_DIM)
H_B = D_MODEL // (2 * HEAD_DIM)
H_C = D_MODEL // HEAD_DIM
ROT_DIM = HEAD_DIM // 4
ROPE_THETA = 500000.0
RET_THETA = 10000.0
MOBA_BLOCK = 256
MOBA_TOPK = 3
MOBA_Q_CHUNK = 64
RET_CHUNK = 128
DILATED = ((128, 1), (512, 4), (2048, 16))
W_MAX = max(w for w, _ in DILATED)
DIL_Q_CHUNK = 64
D_FF = ((8 * D_MODEL + 3 * 256 - 1) // (3 * 256)) * 256
N_AB = (DEPTH + 1) // 2
N_C = DEPTH // 2
EPS = 1e-6
NEG_INF = -1e30

kernel_name = 'moba_retnet_longnet_hybrid_step'


def rms_norm(x, w):
    x32 = x.astype(jnp.float32)
    y = x32 * lax.rsqrt(jnp.mean(x32 * x32, axis=-1, keepdims=True) + EPS)
    return (y * w.astype(jnp.float32)).astype(x.dtype)


def rope(x, pos, rot_dim, theta):
    half = rot_dim // 2
    inv = theta ** (-jnp.arange(half, dtype=jnp.float32) / half)
    ang = pos.astype(jnp.float32)[:, None] * inv[None, :]
    cos = jnp.cos(ang)[:, None, :].astype(x.dtype)
    sin = jnp.sin(ang)[:, None, :].astype(x.dtype)
    x1 = x[..., :half]
    x2 = x[..., half:rot_dim]
    return jnp.concatenate([x1 * cos - x2 * sin, x1 * sin + x2 * cos, x[..., rot_dim:]], axis=-1)


def moba_attention(q, k, v, q_pos):
    B, L, H, dh = k.shape
    n_blk = max(-(-L // MOBA_BLOCK), MOBA_TOPK)
    pad = n_blk * MOBA_BLOCK - L
    k = jnp.pad(k, ((0, 0), (0, pad), (0, 0), (0, 0)))
    v = jnp.pad(v, ((0, 0), (0, pad), (0, 0), (0, 0)))
    kb = k.reshape(B, n_blk, MOBA_BLOCK, H, dh).transpose(0, 3, 1, 2, 4)
    vb = v.reshape(B, n_blk, MOBA_BLOCK, H, dh).transpose(0, 3, 1, 2, 4)
    k_mean = jnp.mean(kb.astype(jnp.float32), axis=3)
    Tq = q.shape[1]
    qc = math.gcd(Tq, MOBA_Q_CHUNK)
    nc = Tq // qc
    q_chunks = q.reshape(B, nc, qc, H, dh).transpose(1, 0, 2, 3, 4)
    pos_chunks = q_pos.reshape(nc, qc)
    b_idx = jnp.arange(B)[:, None, None, None]
    h_idx = jnp.arange(H)[None, :, None, None]
    blk_ids = jnp.arange(n_blk)
    in_blk = jnp.arange(MOBA_BLOCK)
    scale = dh ** -0.5

    def one_chunk(args):
        qq, pp = args
        own = pp // MOBA_BLOCK
        s_blk = jnp.einsum('bqhd,bhnd->bhqn', qq.astype(jnp.float32), k_mean)
        fully_past = blk_ids[None, :] < own[:, None]
        s_blk = jnp.where(fully_past[None, None], s_blk, NEG_INF)
        _, top_idx = lax.top_k(s_blk, MOBA_TOPK)
        own_b = jnp.broadcast_to(own[None, None, :, None], (B, H, qc, 1)).astype(top_idx.dtype)
        idx = jnp.concatenate([top_idx, own_b], axis=-1)
        k_sel = kb[b_idx, h_idx, idx]
        v_sel = vb[b_idx, h_idx, idx]
        s = jnp.einsum('bqhd,bhqnkd->bhqnk', qq, k_sel).astype(jnp.float32) * scale
        key_pos = idx[..., None] * MOBA_BLOCK + in_blk
        sel_ok = jnp.concatenate([jnp.arange(MOBA_TOPK)[None, :] < own[:, None],
                                  jnp.ones((qc, 1), dtype=bool)], axis=-1)
        ok = sel_ok[None, None, :, :, None] & (key_pos <= pp[None, None, :, None, None])
        s = jnp.where(ok, s, NEG_INF)
        p = jax.nn.softmax(s.reshape(B, H, qc, -1), axis=-1).reshape(s.shape)
        return jnp.einsum('bhqnk,bhqnkd->bqhd', p.astype(v_sel.dtype), v_sel)

    out = lax.map(one_chunk, (q_chunks, pos_chunks))
    return out.transpose(1, 0, 2, 3, 4).reshape(B, Tq, H, dh)


def retention(q, k, v, s0):
    B, T, H, dk = q.shape
    dv = v.shape[-1]
    L = math.gcd(T, RET_CHUNK)
    nc = T // L
    log_g = jnp.log1p(-jnp.exp2(-5.0 - jnp.arange(H, dtype=jnp.float32)))
    i = jnp.arange(L, dtype=jnp.float32)
    diff = i[:, None] - i[None, :]
    causal = diff >= 0
    decay = jnp.where(causal[None], jnp.exp(jnp.where(causal, diff, 0.0)[None] * log_g[:, None, None]), 0.0).astype(q.dtype)
    q_dec = jnp.exp((i + 1.0)[:, None] * log_g[None, :]).astype(q.dtype)[None, :, :, None]
    k_dec = jnp.exp((L - 1.0 - i)[:, None] * log_g[None, :]).astype(q.dtype)[None, :, :, None]
    c_dec = jnp.exp(L * log_g).astype(q.dtype)[None, :, None, None]

    def split(x):
        return x.reshape(B, nc, L, H, x.shape[-1]).transpose(1, 0, 2, 3, 4)

    def step(state, xs):
        qc, kc, vc = xs
        attn = jnp.einsum('bihd,bjhd->bhij', qc, kc) * decay[None]
        inner = jnp.einsum('bhij,bjhe->bihe', attn, vc)
        cross = jnp.einsum('bihd,bhde->bihe', qc, state) * q_dec
        new = state * c_dec + jnp.einsum('bjhd,bjhe->bhde', kc * k_dec, vc)
        return new.astype(state.dtype), (inner + cross).astype(vc.dtype)

    s_final, out = lax.scan(step, s0, (split(q), split(k), split(v)))
    return out.transpose(1, 0, 2, 3, 4).reshape(B, T, H, dv), s_final


def head_norm(o, gain):
    o32 = o.astype(jnp.float32)
    mu = jnp.mean(o32, axis=-1, keepdims=True)
    var = jnp.mean(jnp.square(o32 - mu), axis=-1, keepdims=True)
    y = (o32 - mu) * lax.rsqrt(var + EPS) * gain.astype(jnp.float32).reshape(o.shape[2], o.shape[3])
    return y.astype(o.dtype)


def dilated_attention(q, k_ext, v_ext, first_valid):
    B, T, H, dh = q.shape
    qc = math.gcd(T, DIL_Q_CHUNK)
    nc = T // qc
    q_chunks = q.reshape(B, nc, qc, H, dh).transpose(1, 0, 2, 3, 4)
    starts = jnp.arange(nc, dtype=jnp.int32) * qc
    scale = dh ** -0.5

    def one_chunk(args):
        qq, start = args
        kk = lax.dynamic_slice_in_dim(k_ext, start, qc + W_MAX, axis=1)
        vv = lax.dynamic_slice_in_dim(v_ext, start, qc + W_MAX, axis=1)
        outs, lses = [], []
        for window, dil in DILATED:
            local = (np.arange(qc)[:, None] + W_MAX - np.arange(window // dil + 1)[None, :] * dil).astype(np.int32)
            ks = kk[:, local]
            vs = vv[:, local]
            s = jnp.einsum('bqhd,bqmhd->bhqm', qq, ks).astype(jnp.float32) * scale
            ok = (start + local) >= first_valid
            s = jnp.where(ok[None, None], s, NEG_INF)
            lse = jax.nn.logsumexp(s, axis=-1)
            p = jnp.exp(s - lse[..., None])
            outs.append(jnp.einsum('bhqm,bqmhd->bqhd', p.astype(vs.dtype), vs))
            lses.append(lse)
        wts = jax.nn.softmax(jnp.stack(lses), axis=0)
        wts = wts.transpose(0, 1, 3, 2)[..., None]
        return jnp.sum(jnp.stack(outs) * wts.astype(outs[0].dtype), axis=0)

    out = lax.map(one_chunk, (q_chunks, starts))
    return out.transpose(1, 0, 2, 3, 4).reshape(B, T, H, dh)


def ab_mixer(h, pos, k_past, v_past, s0, w_in, w_out, gn_w):
    B, T, _ = h.shape
    wa = H_A * HEAD_DIM
    wb = H_B * HEAD_DIM
    proj = h @ w_in
    qa, ka, va, qb, kb, vb, gb = jnp.split(proj, [wa, 2 * wa, 3 * wa, 3 * wa + wb, 3 * wa + 2 * wb, 3 * wa + 3 * wb], axis=-1)
    qa = rope(qa.reshape(B, T, H_A, HEAD_DIM), pos, ROT_DIM, ROPE_THETA)
    ka = rope(ka.reshape(B, T, H_A, HEAD_DIM), pos, ROT_DIM, ROPE_THETA)
    va = va.reshape(B, T, H_A, HEAD_DIM)
    if k_past is None:
        k_all, v_all = ka, va
    else:
        k_all = jnp.concatenate([k_past, ka], axis=1)
        v_all = jnp.concatenate([v_past, va], axis=1)
    oa = moba_attention(qa, k_all, v_all, pos)
    qb = rope(qb.reshape(B, T, H_B, HEAD_DIM), pos, HEAD_DIM, RET_THETA)
    kb = rope(kb.reshape(B, T, H_B, HEAD_DIM), pos, HEAD_DIM, RET_THETA) * (HEAD_DIM ** -0.5)
    ob, s_new = retention(qb, kb, vb.reshape(B, T, H_B, HEAD_DIM), s0)
    ob = head_norm(ob, gn_w).reshape(B, T, wb) * jax.nn.silu(gb)
    out = jnp.concatenate([oa.reshape(B, T, wa), ob], axis=-1) @ w_out
    return out, ka, va, s_new


def c_mixer(h, pos, k_buf, v_buf, w_in, w_out):
    B, T, _ = h.shape
    q, k, v = jnp.split(h @ w_in, 3, axis=-1)
    q = rope(q.reshape(B, T, H_C, HEAD_DIM), pos, ROT_DIM, ROPE_THETA)
    k = rope(k.reshape(B, T, H_C, HEAD_DIM), pos, ROT_DIM, ROPE_THETA)
    v = v.reshape(B, T, H_C, HEAD_DIM)
    if k_buf is None:
        ctx_k, ctx_v, n_prev = k, v, 0
    else:
        ctx_k = jnp.concatenate([k_buf, k], axis=1)
        ctx_v = jnp.concatenate([v_buf, v], axis=1)
        n_prev = k_buf.shape[1]
    lead = W_MAX - n_prev
    k_ext = jnp.pad(ctx_k, ((0, 0), (lead, 0), (0, 0), (0, 0)))
    v_ext = jnp.pad(ctx_v, ((0, 0), (lead, 0), (0, 0), (0, 0)))
    o = dilated_attention(q, k_ext, v_ext, lead)
    keep = min(W_MAX, n_prev + T)
    return o.reshape(B, T, H_C * HEAD_DIM) @ w_out, ctx_k[:, -keep:], ctx_v[:, -keep:]


def swiglu_ffn(h, w_gate, w_up, w_down):
    return (jax.nn.silu(h @ w_gate) * (h @ w_up)) @ w_down


def setup_inputs(seed: int = 0) -> dict:
    key = jax.random.key(seed)
    ks = jax.random.split(key, 24)
    f32 = jnp.float32
    n_pages = PAST_LEN // PAGE_SIZE
    n_used = DEC_BATCH * n_pages
    n_pool = n_used + n_used // 4
    wbuf = min(W_MAX, PAST_LEN)
    d_ab_in = 3 * H_A * HEAD_DIM + 4 * H_B * HEAD_DIM
    d_ab_out = H_A * HEAD_DIM + H_B * HEAD_DIM
    d_c = H_C * HEAD_DIM

    def nrm(k, shape, scale=1.0):
        return jax.random.normal(k, shape, f32) * scale

    return {
        'x_prompt': nrm(ks[0], (BATCH, SEQ, D_MODEL)),
        'x_sample': nrm(ks[1], (DEC_BATCH, DEC_SEQ, D_MODEL)),
        'cache_k_a': nrm(ks[2], (N_AB, n_pool, PAGE_SIZE, H_A, HEAD_DIM)),
        'cache_v_a': nrm(ks[3], (N_AB, n_pool, PAGE_SIZE, H_A, HEAD_DIM)),
        'page_table': jax.random.permutation(ks[4], n_pool)[:n_used].reshape(DEC_BATCH, n_pages).astype(jnp.int32),
        'state_ret': nrm(ks[5], (N_AB, DEC_BATCH, H_B, HEAD_DIM, HEAD_DIM), 0.1),
        'cache_win_k': nrm(ks[6], (N_C, DEC_BATCH, wbuf, H_C, HEAD_DIM)),
        'cache_win_v': nrm(ks[7], (N_C, DEC_BATCH, wbuf, H_C, HEAD_DIM)),
        'norm_mix': 1.0 + nrm(ks[8], (DEPTH, D_MODEL), 0.02),
        'norm_ffn': 1.0 + nrm(ks[9], (DEPTH, D_MODEL), 0.02),
        'norm_final': 1.0 + nrm(ks[10], (D_MODEL,), 0.02),
        'w_in_ab': nrm(ks[11], (N_AB, D_MODEL, d_ab_in), D_MODEL ** -0.5),
        'w_out_ab': nrm(ks[12], (N_AB, d_ab_out, D_MODEL), d_ab_out ** -0.5),
        'ret_gn_w': 1.0 + nrm(ks[13], (N_AB, H_B * HEAD_DIM), 0.02),
        'w_in_c': nrm(ks[14], (N_C, D_MODEL, 3 * d_c), D_MODEL ** -0.5),
        'w_out_c': nrm(ks[15], (N_C, d_c, D_MODEL), d_c ** -0.5),
        'ffn_w_gate': nrm(ks[16], (DEPTH, D_MODEL, D_FF), D_MODEL ** -0.5),
        'ffn_w_up': nrm(ks[17], (DEPTH, D_MODEL, D_FF), D_MODEL ** -0.5),
        'ffn_w_down': nrm(ks[18], (DEPTH, D_FF, D_MODEL), D_FF ** -0.5),
    }


def reference(x_prompt, x_sample, cache_k_a, cache_v_a, page_table, state_ret, cache_win_k, cache_win_v,
              norm_mix, norm_ffn, norm_final, w_in_ab, w_out_ab, ret_gn_w, w_in_c, w_out_c,
              ffn_w_gate, ffn_w_up, ffn_w_down):
    n_seq, n_pages = page_table.shape
    past_len = n_pages * cache_k_a.shape[2]
    t_p = x_prompt.shape[1]
    t_s = x_sample.shape[1]
    pos_p = jnp.arange(t_p, dtype=jnp.int32)
    pos_s = past_len + jnp.arange(t_s, dtype=jnp.int32)
    xp, xs = x_prompt, x_sample
    ka_p, va_p, ret_p, kc_p, vc_p = [], [], [], [], []
    ka_s, va_s, ret_s, kc_s, vc_s = [], [], [], [], []
    for l in range(DEPTH):
        i = l // 2
        hp = rms_norm(xp, norm_mix[l])
        hs = rms_norm(xs, norm_mix[l])
        if l % 2 == 0:
            k_past = cache_k_a[i][page_table].reshape(n_seq, past_len, H_A, HEAD_DIM)
            v_past = cache_v_a[i][page_table].reshape(n_seq, past_len, H_A, HEAD_DIM)
            s0 = jnp.zeros((xp.shape[0], H_B, HEAD_DIM, HEAD_DIM), state_ret.dtype)
            op, kp, vp, sp = ab_mixer(hp, pos_p, None, None, s0, w_in_ab[i], w_out_ab[i], ret_gn_w[i])
            os_, kss, vss, sss = ab_mixer(hs, pos_s, k_past, v_past, state_ret[i], w_in_ab[i], w_out_ab[i], ret_gn_w[i])
            ka_p.append(kp); va_p.append(vp); ret_p.append(sp)
            ka_s.append(kss); va_s.append(vss); ret_s.append(sss)
        else:
            op, kp, vp = c_mixer(hp, pos_p, None, None, w_in_c[i], w_out_c[i])
            os_, kss, vss = c_mixer(hs, pos_s, cache_win_k[i], cache_win_v[i], w_in_c[i], w_out_c[i])
            kc_p.append(kp); vc_p.append(vp)
            kc_s.append(kss); vc_s.append(vss)
        xp = xp + op
        xs = xs + os_
        xp = xp + swiglu_ffn(rms_norm(xp, norm_ffn[l]), ffn_w_gate[l], ffn_w_up[l], ffn_w_down[l])
        xs = xs + swiglu_ffn(rms_norm(xs, norm_ffn[l]), ffn_w_gate[l], ffn_w_up[l], ffn_w_down[l])
    y_prompt = rms_norm(xp, norm_final)
    y_sample = rms_norm(xs, norm_final)
    return (y_prompt, y_sample,
            jnp.stack(ka_p), jnp.stack(va_p), jnp.stack(ret_p), jnp.stack(kc_p), jnp.stack(vc_p),
            jnp.stack(ka_s), jnp.stack(va_s), jnp.stack(ret_s), jnp.stack(kc_s), jnp.stack(vc_s))
```

```python
import math
import numpy as np
import concourse.bass as bass
import concourse.mybir as mybir
from concourse.bass_utils import run_bass_kernel_spmd

F32 = mybir.dt.float32
BF16 = mybir.dt.bfloat16
I32 = mybir.dt.int32
ALU = mybir.AluOpType
AF = mybir.ActivationFunctionType
AX = mybir.AxisListType

SEM_LIMIT = 30000
DMA_SEM_LIMIT = 2000
D = 1024
HD = 64
EPS = 1e-6
W_MAX = 2048
DFF = 2816
BIGB = 256.0


class Rec:
    __slots__ = ("eng", "fn", "dma", "deps", "sig", "pos", "semval", "hard")


class Prog:
    ENGS = ("pe", "act", "dve", "pool", "sp")

    def __init__(self, nc):
        self.nc = nc
        self.recs = []
        self.last_w = {}
        self.readers = {}
        self.eng_last = {}
        self.stream_last = {}
        self.pending = {e: set() for e in self.ENGS}
        self.eng_count = {e: 0 for e in self.ENGS}

    def op(self, eng, fn, reads=(), writes=(), dma=None):
        r = Rec()
        idx = len(self.recs)
        qprev = None
        if dma is not None:
            dma = eng + "_q"
            qprev = self.stream_last.get(dma)
        r.eng, r.fn, r.dma, r.sig, r.semval = eng, fn, dma, False, None
        pr = [k for k in reads if isinstance(k, str) and k.startswith("ps_")]
        if pr:
            reads = [k for k in reads if k not in pr]
            writes = list(writes) + pr
        deps = set(self.pending[eng])
        self.pending[eng] = set()
        hard = set(deps)
        for k in reads:
            w = self.last_w.get(k)
            if w is not None:
                deps.add(w)
                hard.add(w)
        for k in writes:
            w = self.last_w.get(k)
            if w is not None:
                deps.add(w)
            rd = self.readers.get(k)
            if rd:
                deps.update(rd.values())
                hard.update(rd.values())
        if qprev is not None:
            deps.add(qprev)
            hard.add(qprev)
        r.deps = deps
        r.hard = hard
        r.pos = self.eng_count[eng]
        self.eng_count[eng] += 1
        self.recs.append(r)
        who = ("dma", dma) if dma is not None else eng
        for k in reads:
            self.readers.setdefault(k, {})[who] = idx
        for k in writes:
            self.last_w[k] = idx
            self.readers[k] = {}
        if dma is not None:
            self.stream_last[dma] = idx
        else:
            self.eng_last[eng] = idx
        return idx

    def barrier(self):
        alls = set(self.eng_last.values()) | set(self.stream_last.values())
        for e in self.ENGS:
            self.pending[e] |= alls

    def finish(self):
        self.barrier()
        for e in self.ENGS:
            self.op(e, None)

    def emit(self):
        nc = self.nc
        recs = self.recs
        need = []
        for i, r in enumerate(recs):
            nl = []
            for d in r.deps:
                p = recs[d]
                if p.fn is None:
                    continue
                if p.dma is not None and p.dma == r.dma and d not in r.hard:
                    continue
                if p.dma is None and r.dma is None and p.eng == r.eng:
                    if r.eng == "pe":
                        continue
                    if r.pos - p.pos > 3:
                        continue
                p.sig = True
                nl.append(d)
            need.append(nl)
        sems = {}
        ctx = []

        def get_sem(key):
            if key not in sems:
                g = nc.semaphore("s_%s_%s_%d" % (key[0], key[1], key[2]))
                ctx.append(g)
                sems[key] = g.__enter__()
            return sems[key]

        cnt = {}
        for r in recs:
            if r.fn is None:
                continue
            if r.dma is not None:
                k = ("d", r.dma)
                step = 16
            elif r.sig:
                k = ("e", r.eng)
                step = 1
            else:
                continue
            ep, v = cnt.get(k, (0, 0))
            if v + step > (DMA_SEM_LIMIT if step == 16 else SEM_LIMIT):
                ep, v = ep + 1, 0
            v += step
            cnt[k] = (ep, v)
            r.semval = ((k[0], k[1], ep), v)
        per_eng = {e: [] for e in self.ENGS}
        for i, r in enumerate(recs):
            per_eng[r.eng].append(i)

        def run_engine(ename, eobj):
            seen = {}
            for i in per_eng[ename]:
                r = recs[i]
                waits = {}
                for d in need[i]:
                    sk, v = recs[d].semval
                    if seen.get(sk, 0) >= v:
                        continue
                    if waits.get(sk, 0) < v:
                        waits[sk] = v
                for sk, v in waits.items():
                    eobj.wait_ge(get_sem(sk), v)
                    seen[sk] = v
                if r.fn is None:
                    continue
                ins = r.fn(eobj)
                if r.semval is not None:
                    sk, v = r.semval
                    ins.then_inc(get_sem(sk), 16 if r.dma is not None else 1)

        for r in recs:
            if r.semval is not None:
                get_sem(r.semval[0])
        with nc.Block() as block:
            @block.tensor
            def _(e):
                run_engine("pe", e)

            @block.scalar
            def _(e):
                run_engine("act", e)

            @block.vector
            def _(e):
                run_engine("dve", e)

            @block.gpsimd
            def _(e):
                run_engine("pool", e)

            @block.sync
            def _(e):
                run_engine("sp", e)
        for g in reversed(ctx):
            g.__exit__(None, None, None)
        self.nsems = len(sems)
        return len(recs)


class SB:
    def __init__(self, nc, base=16640, limit=229376):
        self.nc = nc
        self.off = base
        self.limit = limit
        self.n = 0

    def alloc(self, name, shape, dtype):
        esz = 4 if dtype in (F32, I32) else 2
        nbytes = int(np.prod(shape[1:])) * esz
        nbytes = (nbytes + 63) // 64 * 64
        assert self.off + nbytes <= self.limit, (name, self.off, nbytes)
        self.n += 1
        t = self.nc.alloc_sbuf_tensor_at("%s_%d" % (name, self.n), list(shape), dtype, offset=self.off)
        self.off += nbytes
        return t

    def mark(self):
        return self.off

    def reset(self, m):
        self.off = m


def rope_tables(pos, half, theta):
    inv = (np.float32(theta) ** (-(np.arange(half, dtype=np.float32)) / np.float32(half))).astype(np.float32)
    ang = pos.astype(np.float32)[:, None] * inv[None, :]
    return np.cos(ang).astype(np.float32), np.sin(ang).astype(np.float32)


def make_consts(T, PAST):
    NT = T // 128
    c = {}
    c["ident"] = np.eye(128, dtype=np.float32)
    j = np.arange(128)
    c["tri"] = (j[None, :] >= j[:, None]).astype(np.float32)
    pos_p = np.arange(T)
    pos_s = PAST + np.arange(8)
    ca, sa = rope_tables(pos_p, 8, 500000.0)
    cb, sb_ = rope_tables(pos_p, 32, 10000.0)
    c["cosA"] = ca.reshape(NT, 128, 8).transpose(1, 0, 2).copy()
    c["sinA"] = sa.reshape(NT, 128, 8).transpose(1, 0, 2).copy()
    c["cosB"] = cb.reshape(NT, 128, 32).transpose(1, 0, 2).copy()
    c["sinB"] = sb_.reshape(NT, 128, 32).transpose(1, 0, 2).copy()
    cas, sas = rope_tables(pos_s, 8, 500000.0)
    cbs, sbs = rope_tables(pos_s, 32, 10000.0)
    c["cosAs"] = np.tile(cas, (4, 1))[:, None, :].copy()
    c["sinAs"] = np.tile(sas, (4, 1))[:, None, :].copy()
    c["cosBs"] = np.tile(cbs, (4, 1))[:, None, :].copy()
    c["sinBs"] = np.tile(sbs, (4, 1))[:, None, :].copy()
    lg = np.log1p(-np.exp2(-5.0 - np.arange(8, dtype=np.float32))).astype(np.float32)
    p = np.arange(128, dtype=np.float32)
    dec = np.zeros((128, 3, 8), np.float32)
    dec[:, 0, :] = np.exp((p[:, None] + 1.0) * lg[None, :])
    dec[:, 1, :] = np.exp(-(p[:, None] + 1.0) * lg[None, :]) * 0.125
    dec[:, 2, :] = np.exp((127.0 - p[:, None]) * lg[None, :]) * 0.125
    c["dec"] = dec
    ps = (np.arange(32) % 8).astype(np.float32)
    decs = np.zeros((32, 3, 8), np.float32)
    decs[:, 0, :] = np.exp((ps[:, None] + 1.0) * lg[None, :])
    decs[:, 1, :] = np.exp(-(ps[:, None] + 1.0) * lg[None, :]) * 0.125
    decs[:, 2, :] = np.exp((7.0 - ps[:, None]) * lg[None, :]) * 0.125
    c["decs"] = decs
    c["cdec"] = np.broadcast_to(np.exp(128.0 * lg)[None, :, None], (64, 8, 64)).astype(np.float32).copy()
    c["cdec8"] = np.broadcast_to(np.exp(8.0 * lg)[None, :, None], (64, 8, 64)).astype(np.float32).copy()
    sq = np.arange(32) // 8
    c["tris"] = ((sq[None, :] == sq[:, None]) & (np.arange(32)[None, :] >= np.arange(32)[:, None])).astype(np.float32)
    c["seqmask"] = (sq[:, None] == np.arange(4)[None, :]).astype(np.float32)
    NB = T // 256
    NBp = max(NB, 8)
    indc = np.zeros((NBp, T), np.float32)
    for n_ in range(NB):
        indc[n_, n_ * 256:(n_ + 1) * 256] = 1.0
    c["indc"] = indc
    own = (np.arange(NT) // 2)
    nn = np.arange(NBp)
    mp = np.where(nn[None, :] < own[:, None], 0.0, -1.0e4).astype(np.float32)
    om = np.where(nn[None, :] == own[:, None], 0.0, -BIGB).astype(np.float32)
    c["Mpast"] = np.broadcast_to(mp[None], (128, NT, NBp)).copy()
    c["OwnM"] = np.broadcast_to(om[None], (128, NT, NBp)).copy()
    def cmult(dl):
        dl = np.asarray(dl)
        return (((dl >= 0) & (dl <= 128)).astype(np.float32) + ((dl >= 0) & (dl <= 512) & (dl % 4 == 0)).astype(np.float32)
                + ((dl >= 0) & (dl <= 2048) & (dl % 16 == 0)).astype(np.float32))
    kj = np.arange(128)[:, None]
    qi = np.arange(512)[None, :]
    c["dilm"] = np.stack([cmult(128 * ee + qi - kj) for ee in range(-3, 17)]).astype(np.float32)
    qi8 = np.arange(8)[None, :]
    dm_ = [cmult(W_MAX + qi8 - (ch * 128 + kj)) for ch in range(W_MAX // 128)]
    own_ = np.zeros((128, 8), np.float32)
    own_[0:8] = cmult(np.arange(8)[None, :] - np.arange(8)[:, None])
    c["dilms"] = np.stack(dm_ + [own_]).astype(np.float32)
    om_ = np.zeros((128, 8, 8), np.float32)
    om_[0:8] = (np.arange(8)[None, :] >= np.arange(8)[:, None]).astype(np.float32)[:, None, :]
    c["ownmask"] = om_.reshape(128, 64)
    pp_ = np.arange(128)
    c["gcon"] = np.stack([np.where(pp_ < 64, 64.0, 0.0), np.where(pp_ >= 64, 64.0, 0.0), (pp_ % 64).astype(np.float64)], axis=1).astype(np.float32)
    dm2 = np.zeros((64, 8, 65), np.float32)
    for h_ in range(8):
        dm2[h_ * 8:(h_ + 1) * 8, h_, :] = 1.0
    c["dmask"] = dm2
    c["dilown"] = cmult(np.arange(8)[None, :] - np.arange(8)[:, None]).astype(np.float32)
    return c


CONST_SHAPES = None


class K:
    pass


def build(T, NPG, NPOOL, debug=False, opts=None):
    opts = opts or {}
    NT = T // 128
    TS = 32
    PAST = NPG * 128
    WK = min(W_MAX, T)
    nc = bass.Bass("TRN2", target_bir_lowering=False)
    P = Prog(nc)
    sb = SB(nc)
    k = K()

    def din(name, shape, dt=F32):
        return nc.dram_tensor(name, list(shape), dt, kind="ExternalInput").ap()

    def dout(name, shape, dt=F32):
        return nc.dram_tensor(name, list(shape), dt, kind="ExternalOutput").ap()

    def dscr(name, shape, dt=BF16):
        if debug:
            return nc.dram_tensor(name, list(shape), dt, kind="ExternalOutput").ap()
        return nc.dram_tensor(name, list(shape), dt).ap()

    xp = din("xp", [T, D])
    xs = din("xs", [TS, D])
    norm_mix = din("norm_mix", [2, D])
    norm_ffn = din("norm_ffn", [2, D])
    norm_final = din("norm_final", [D])
    w_in_ab = din("w_in_ab", [D, 3584])
    w_out_ab = din("w_out_ab", [D, D])
    ret_gn_w = din("ret_gn_w", [512])
    state_ret = din("state_ret", [4, 8, 64, 64])
    ffn_w_gate = din("ffn_w_gate", [2, D, DFF])
    ffn_w_up = din("ffn_w_up", [2, D, DFF])
    ffn_w_down = din("ffn_w_down", [2, DFF, D])
    x1scr = dscr("x1scr", [T + 32, D], F32)
    w_in_c = din("w_in_c", [D, 3072])
    w_out_c = din("w_out_c", [D, D])
    cwk = din("cwk", [4, W_MAX, D])
    cwv = din("cwv", [4, W_MAX, D])
    y_p = dout("y_p", [T, D])
    y_s = dout("y_s", [TS, D])
    kc_p = dout("kc_p", [WK, D])
    vc_p = dout("vc_p", [WK, D])
    kc_s = dout("kc_s", [4, W_MAX, D])
    vc_s = dout("vc_s", [4, W_MAX, D])
    qcT = dscr("qcT", [8, 128, T])
    kcT = dscr("kcT", [8, 128, T])
    vc16 = dscr("vc16", [16, 128, NT, 65])
    o1T = dscr("o1T", [D, T + TS])
    qcTs = dscr("qcTs", [8, 128, TS])
    kcTs = dscr("kcTs", [8, 128, TS])
    vc16s = dscr("vc16s", [TS, 16, 65])
    cache_k2 = din("cache_k2", [NPOOL * 64, 1024])
    cache_v2 = din("cache_v2", [NPOOL * 64, 1024])
    page_table = din("page_table", [4, NPG], I32)
    consts = make_consts(T, PAST)
    cin = {n: din("c_" + n, v.shape) for n, v in consts.items()}
    cin_i = {"iota_p": din("ci_iota_p", [128, 1], F32)}
    selscr = dscr("selscr", [4, NPG // 2 + 1, 64], F32)
    ka_p = dout("ka_p", [T, 512])
    va_p = dout("va_p", [T, 512])
    ret_p = dout("ret_p", [8, 64, 64])
    ka_s = dout("ka_s", [TS, 512])
    va_s = dout("va_s", [TS, 512])
    ret_s = dout("ret_s", [4, 8, 64, 64])
    qaT = dscr("qaT", [4, 128, T])
    kaT = dscr("kaT", [4, 128, T])
    va16 = dscr("va16", [8, 128, NT, 65])
    oT = dscr("oT", [D, T + TS])
    qaTs = dscr("qaTs", [4, 128, TS])
    kaTs = dscr("kaTs", [4, 128, TS])
    va16s = dscr("va16s", [TS, 8, 65])

    pa = nc.alloc_psum_tensor("pa", [128, 1024], F32)
    pb = nc.alloc_psum_tensor("pb", [128, 1024], F32)
    pT = nc.alloc_psum_tensor("pT", [128, 8, 128], BF16)
    pc = nc.alloc_psum_tensor("pc", [128, 1024], F32)

    ident = sb.alloc("ident", [128, 128], BF16)
    tri = sb.alloc("tri", [128, 128], BF16)
    P.op("pool", lambda e: e.dma_start(out=ident[:], in_=cin["ident"]), writes=["ident"], dma="ident")
    P.op("pool", lambda e: e.dma_start(out=tri[:], in_=cin["tri"]), writes=["tri"], dma="tri")
    gmark = sb.mark()

    def load_w_bf16(dst, src_ap, kchunks, ncols, key, colchunk=512):
        v = src_ap.rearrange("(k p) n -> p k n", p=128)
        for c0 in range(0, ncols, colchunk):
            c1 = min(ncols, c0 + colchunk)
            P.op("pool", lambda e, c0=c0, c1=c1: e.dma_start(out=dst[:, :, c0:c1], in_=v[:, :, c0:c1]),
                 writes=[key], dma=key)

    def rmsnorm_hT(xt, n, nwb, s, tag):
        ssq, rstd, junk, h16, hT = k.ssq[s], k.rstd[s], k.junk, k.h16[s], k.hT[s]
        kx = "xt%d" % s
        P.op("act", lambda e: e.activation(out=junk[:n], in_=xt[:n], func=AF.Square, accum_out=ssq[:n]),
             reads=[kx], writes=["junk", "ssq%d" % s])
        P.op("act", lambda e: e.activation(out=rstd[:n], in_=ssq[:n], func=AF.Sqrt, bias=EPS, scale=1.0 / D),
             reads=["ssq%d" % s], writes=["rstd%d" % s])
        P.op("dve", lambda e: e.reciprocal(out=rstd[:n], in_=rstd[:n]), reads=["rstd%d" % s], writes=["rstd%d" % s])
        P.op("dve", lambda e: e.scalar_tensor_tensor(out=h16[:n], in0=xt[:n], scalar=rstd[:n, 0:1], in1=nwb[:n],
                                                     op0=ALU.mult, op1=ALU.mult),
             reads=[kx, "rstd%d" % s, tag], writes=["h16%d" % s])
        for c in range(8):
            P.op("pe", lambda e, c=c: e.transpose(out=pT[:, c, :n], in_=h16[:n, c * 128:(c + 1) * 128], identity=ident[:n, :n]),
                 reads=["h16%d" % s, "ident"], writes=["ps_pT"])
        P.op("act", lambda e: e.copy(out=hT[:, :, :n], in_=pT[:, :, :n]), reads=["ps_pT"], writes=["hT%d" % s])
        return hT, "hT%d" % s

    def proj(ps, pkey, hT, hkey, n, w, wkey, c0, ncols, pcol0=0):
        for g0 in range(0, ncols, 512):
            g1 = min(ncols, g0 + 512)
            for c in range(8):
                P.op("pe", lambda e, c=c, g0=g0, g1=g1: e.matmul(out=ps[:n, pcol0 + g0:pcol0 + g1], lhsT=hT[:, c, :n],
                                                                rhs=w[:, c, c0 + g0:c0 + g1], start=(c == 0), stop=(c == 7)),
                     reads=[hkey, wkey], writes=[pkey])

    def rope(ps, pkey, dst, dkey, n, ng, half, cos, sin, tmp):
        pv = ps[:n, 0:ng * 64].rearrange("p (g d) -> p g d", d=64)
        dv = dst[:n, 0:ng * 64].rearrange("p (g d) -> p g d", d=64)
        tv = tmp[:n, 0:ng * half].rearrange("p (g d) -> p g d", d=half)
        cb = cos.unsqueeze(1).to_broadcast([n, ng, half])
        sbb = sin.unsqueeze(1).to_broadcast([n, ng, half])
        if 2 * half < 64:
            P.op("act", lambda e: e.copy(out=dst[:n, 0:ng * 64], in_=ps[:n, 0:ng * 64]), reads=[pkey], writes=[dkey])
        x1, x2 = pv[:, :, 0:half], pv[:, :, half:2 * half]
        d1, d2 = dv[:, :, 0:half], dv[:, :, half:2 * half]
        P.op("dve", lambda e: e.tensor_tensor(out=d1, in0=x1, in1=cb, op=ALU.mult), reads=[pkey, "rope"], writes=[dkey])
        P.op("dve", lambda e: e.tensor_tensor(out=tv, in0=x2, in1=sbb, op=ALU.mult), reads=[pkey, "rope"], writes=["ropetmp"])
        P.op("dve", lambda e: e.tensor_tensor(out=d1, in0=d1, in1=tv, op=ALU.subtract), reads=[dkey, "ropetmp"], writes=[dkey])
        P.op("dve", lambda e: e.tensor_tensor(out=d2, in0=x2, in1=cb, op=ALU.mult), reads=[pkey, "rope"], writes=[dkey])
        P.op("dve", lambda e: e.tensor_tensor(out=tv, in0=x1, in1=sbb, op=ALU.mult), reads=[pkey, "rope"], writes=["ropetmp"])
        P.op("dve", lambda e: e.tensor_tensor(out=d2, in0=d2, in1=tv, op=ALU.add), reads=[dkey, "ropetmp"], writes=[dkey])

    w_in = sb.alloc("w_in", [128, 8, 3584], BF16)
    load_w_bf16(w_in, w_in_ab, 8, 3584, "w_in")
    nwb = sb.alloc("nwb", [128, D], F32)
    P.op("sp", lambda e: e.dma_start(out=nwb[:], in_=norm_mix[0].partition_broadcast(128)), writes=["nwb"], dma="nwb")
    gnw = sb.alloc("gnw", [128, 512], F32)
    P.op("sp", lambda e: e.dma_start(out=gnw[:], in_=ret_gn_w.partition_broadcast(128)), writes=["gnw"], dma="gnw")
    cosA = sb.alloc("cosA", [128, NT, 8], F32)
    sinA = sb.alloc("sinA", [128, NT, 8], F32)
    cosB = sb.alloc("cosB", [128, NT, 32], F32)
    sinB = sb.alloc("sinB", [128, NT, 32], F32)
    cosAs = sb.alloc("cosAs", [32, 1, 8], F32)
    sinAs = sb.alloc("sinAs", [32, 1, 8], F32)
    cosBs = sb.alloc("cosBs", [32, 1, 32], F32)
    sinBs = sb.alloc("sinBs", [32, 1, 32], F32)
    dec = sb.alloc("dec", [128, 3, 8], F32)
    decs = sb.alloc("decs", [32, 3, 8], F32)
    cdec = sb.alloc("cdec", [64, 8, 64], F32)
    cdec8 = sb.alloc("cdec8", [64, 8, 64], F32)
    tris = sb.alloc("tris", [32, 32], BF16)
    seqmask = sb.alloc("seqmask", [32, 4], F32)
    for nm, t in (("cosA", cosA), ("sinA", sinA), ("cosB", cosB), ("sinB", sinB), ("cosAs", cosAs), ("sinAs", sinAs),
                  ("cosBs", cosBs), ("sinBs", sinBs), ("dec", dec), ("decs", decs), ("cdec", cdec), ("cdec8", cdec8),
                  ("seqmask", seqmask)):
        P.op("sp", lambda e, t=t, nm=nm: e.dma_start(out=t[:], in_=cin[nm]), writes=["rope"], dma="rope")
    P.op("pool", lambda e: e.dma_start(out=tris[:], in_=cin["tris"]), writes=["rope"], dma="rope")
    S = sb.alloc("S", [64, 8, 64], F32)
    Sbf = sb.alloc("Sbf", [64, 8, 64], BF16)
    P.op("dve", lambda e: e.memset(S[:], 0.0), writes=["S"])
    P.op("dve", lambda e: e.memset(Sbf[:], 0.0), writes=["Sbf"])
    k.junk = sb.alloc("junk", [128, D], BF16)
    k.ssq = [sb.alloc("ssq", [128, 1], F32) for _ in range(2)]
    k.rstd = [sb.alloc("rstd", [128, 1], F32) for _ in range(2)]
    k.h16 = [sb.alloc("h16", [128, D], BF16) for _ in range(2)]
    k.hT = [sb.alloc("hT", [128, 8, 128], BF16) for _ in range(2)]
    xts = [sb.alloc("xt", [128, D], F32) for _ in range(2)]
    qkf = sb.alloc("qkf", [128, 1024], F32)
    qk16 = sb.alloc("qk16", [128, 1024], BF16)
    qkb = sb.alloc("qkb", [128, 1024], F32)
    ropetmp = sb.alloc("ropetmp", [128, 512], F32)
    q16 = sb.alloc("q16", [128, 512], BF16)
    kT16 = sb.alloc("kT16", [128, 512], BF16)
    kU16 = sb.alloc("kU16", [128, 512], BF16)
    vaf = sb.alloc("vaf", [128, 512], F32)
    vb16 = sb.alloc("vb16", [128, 512], BF16)
    gs = sb.alloc("gs", [128, 512], F32)
    qbT = sb.alloc("qbT", [64, 8, 128], BF16)
    kbT = sb.alloc("kbT", [64, 8, 128], BF16)
    at16 = sb.alloc("at16", [128, 8, 128], BF16)
    obf = sb.alloc("obf", [128, 512], F32)
    cen = sb.alloc("cen", [128, 512], F32)
    sqb = sb.alloc("sqb", [128, 512], F32)
    mu = sb.alloc("mu", [128, 8], F32)
    var = sb.alloc("var", [128, 8], F32)
    ob16 = sb.alloc("ob16", [128, 512], BF16)
    qaT_st = sb.alloc("qaT_st", [128, 4, 512], BF16)
    kaT_st = sb.alloc("kaT_st", [128, 4, 512], BF16)
    oT_st = sb.alloc("oT_st", [128, 4, 512], BF16)
    v_st = sb.alloc("v_st", [128, 8, 4, 65], BF16)
    P.op("dve", lambda e: e.memset(v_st[:], 1.0), writes=["v_st"])
    qbTm = sb.alloc("qbTm", [64, 4, 8, 32], BF16)
    kUm = sb.alloc("kUm", [32, 4, 512], BF16)
    Ss = sb.alloc("Ss", [64, 4, 8, 64], F32)
    Ssbf = sb.alloc("Ssbf", [64, 4, 8, 64], BF16)

    def tile0a(i, n, sample):
        s = i % 2
        xt = xts[s]
        sub = i % 4
        src = xs if sample else xp[i * 128:(i + 1) * 128, :]
        P.op("sp", lambda e: e.dma_start(out=xt[:n], in_=src), writes=["xt%d" % s], dma="xt%d" % s)
        hT, hkey = rmsnorm_hT(xt, n, nwb, s, "nwb")
        if sample:
            cA, sA, cB, sB_, dc = cosAs[:n, 0, :], sinAs[:n, 0, :], cosBs[:n, 0, :], sinBs[:n, 0, :], decs
        else:
            cA, sA, cB, sB_, dc = cosA[:n, i, :], sinA[:n, i, :], cosB[:n, i, :], sinB[:n, i, :], dec
        if opts.get("stage", 99) < 1:
            return
        proj(pa, "ps_pa", hT, hkey, n, w_in, "w_in", 0, 1024)
        if opts.get("stage", 99) < 1.2:
            return
        rope(pa, "ps_pa", qkf, "qkf", n, 16, 8, cA, sA, ropetmp)
        if opts.get("stage", 99) < 1.4:
            return
        dst_k = ka_s if sample else ka_p[i * 128:(i + 1) * 128, :]
        P.op("sp", lambda e: e.dma_start(out=dst_k, in_=qkf[:n, 512:1024]), reads=["qkf"], writes=["ka_s_out"] if sample else [], dma="qkf")
        P.op("act", lambda e: e.copy(out=qk16[:n], in_=qkf[:n]), reads=["qkf"], writes=["qk16"])
        if opts.get("stage", 99) < 1.6:
            return
        for c in range(8):
            P.op("pe", lambda e, c=c: e.transpose(out=pT[:, c, :n], in_=qk16[:n, c * 128:(c + 1) * 128], identity=ident[:n, :n]),
                 reads=["qk16", "ident"], writes=["ps_pT"])
        if sample:
            P.op("act", lambda e: e.copy(out=qaT_st[:, :, 0:n], in_=pT[:, 0:4, :n]), reads=["ps_pT"], writes=["qaT_st"])
            P.op("dve", lambda e: e.tensor_copy(out=kaT_st[:, :, 0:n], in_=pT[:, 4:8, :n]), reads=["ps_pT"], writes=["kaT_st"])
        else:
            P.op("act", lambda e: e.copy(out=qaT_st[:, :, sub * 128:(sub + 1) * 128], in_=pT[:, 0:4, :]), reads=["ps_pT"], writes=["qaT_st"])
            P.op("dve", lambda e: e.tensor_copy(out=kaT_st[:, :, sub * 128:(sub + 1) * 128], in_=pT[:, 4:8, :]), reads=["ps_pT"], writes=["kaT_st"])
        if opts.get("stage", 99) < 2:
            return
        proj(pb, "ps_pb", hT, hkey, n, w_in, "w_in", 1536, 1024)
        rope(pb, "ps_pb", qkb, "qkb", n, 16, 32, cB, sB_, ropetmp)
        qv = qkb[:n, 0:512].rearrange("p (h d) -> p h d", d=64)
        kv = qkb[:n, 512:1024].rearrange("p (h d) -> p h d", d=64)

        def bc(j):
            return dc[:n, j, :].unsqueeze(2).to_broadcast([n, 8, 64])
        P.op("dve", lambda e: e.tensor_tensor(out=q16[:n].rearrange("p (h d) -> p h d", d=64), in0=qv, in1=bc(0), op=ALU.mult),
             reads=["qkb", "rope"], writes=["q16"])
        P.op("dve", lambda e: e.tensor_tensor(out=kT16[:n].rearrange("p (h d) -> p h d", d=64), in0=kv, in1=bc(1), op=ALU.mult),
             reads=["qkb", "rope"], writes=["kT16"])
        P.op("dve", lambda e: e.tensor_tensor(out=kU16[:n].rearrange("p (h d) -> p h d", d=64), in0=kv, in1=bc(2), op=ALU.mult),
             reads=["qkb", "rope"], writes=["kU16"])
        for h in range(8):
            P.op("pe", lambda e, h=h: e.transpose(out=pT[0:64, h, :n], in_=q16[:n, h * 64:(h + 1) * 64], identity=ident[:n, :n]),
                 reads=["q16", "ident"], writes=["ps_pT"])
        P.op("act", lambda e: e.copy(out=qbT[:, :, :n], in_=pT[0:64, :, :n]), reads=["ps_pT"], writes=["qbT"])
        for h in range(8):
            P.op("pe", lambda e, h=h: e.transpose(out=pT[0:64, h, :n], in_=kT16[:n, h * 64:(h + 1) * 64], identity=ident[:n, :n]),
                 reads=["kT16", "ident"], writes=["ps_pT"])
        P.op("act", lambda e: e.copy(out=kbT[:, :, :n], in_=pT[0:64, :, :n]), reads=["ps_pT"], writes=["kbT"])
        if opts.get("stage", 99) < 3:
            return
        proj(pa, "ps_pa", hT, hkey, n, w_in, "w_in", 1024, 512, 0)
        proj(pa, "ps_pa", hT, hkey, n, w_in, "w_in", 2560, 512, 512)
        P.op("act", lambda e: e.copy(out=vaf[:n], in_=pa[:n, 0:512]), reads=["ps_pa"], writes=["vaf"])
        dst_v = va_s if sample else va_p[i * 128:(i + 1) * 128, :]
        P.op("sp", lambda e: e.dma_start(out=dst_v, in_=vaf[:n]), reads=["vaf"], writes=["va_s_out"] if sample else [], dma="vaf")
        vsl = v_st[:n, :, 0, 0:64] if sample else v_st[:, :, sub, 0:64]
        P.op("dve", lambda e: e.tensor_copy(out=vsl, in_=pa[:n, 0:512].rearrange("p (h d) -> p h d", d=64)), reads=["ps_pa"], writes=["v_st"])
        P.op("act", lambda e: e.copy(out=vb16[:n], in_=pa[:n, 512:1024]), reads=["ps_pa"], writes=["vb16"])
        if opts.get("stage", 99) < 4:
            return
        proj(pb, "ps_pb", hT, hkey, n, w_in, "w_in", 3072, 512, 0)
        P.op("act", lambda e: e.activation(out=gs[:n], in_=pb[:n, 0:512], func=AF.Silu), reads=["ps_pb"], writes=["gs"])
        if opts.get("stage", 99) < 5:
            return
        pa3 = pa[:, :].rearrange("p (h i) -> p h i", i=128)
        trim = tris if sample else tri
        for h in range(8):
            P.op("pe", lambda e, h=h: e.matmul(out=pa3[:n, h, :n], lhsT=kbT[:, h, :n], rhs=qbT[:, h, :n], start=True, stop=True),
                 reads=["kbT", "qbT"], writes=["ps_pa"])
        P.op("dve", lambda e: e.tensor_tensor(out=at16[:n, :, :n], in0=pa3[:n, :, :n], in1=trim[:n, :n].unsqueeze(1).to_broadcast([n, 8, n]),
                                              op=ALU.mult), reads=["ps_pa", "tri", "rope"], writes=["at16"])
        if sample:
            for sq in range(4):
                P.op("dve", lambda e, sq=sq: e.memset(qbTm[:, sq, :, :], 0.0), writes=["qbTm"])
                P.op("dve", lambda e, sq=sq: e.tensor_copy(out=qbTm[:, sq, :, sq * 8:(sq + 1) * 8], in_=qbT[:, :, sq * 8:(sq + 1) * 8]),
                     reads=["qbT"], writes=["qbTm"])
                P.op("dve", lambda e, sq=sq: e.tensor_scalar(out=kUm[:n, sq, :], in0=kU16[:n], scalar1=seqmask[:n, sq:sq + 1], scalar2=None,
                                                           op0=ALU.mult), reads=["kU16", "rope"], writes=["kUm"])
        for h in range(8):
            P.op("pe", lambda e, h=h: e.matmul(out=pb[:n, h * 64:(h + 1) * 64], lhsT=at16[:n, h, :n], rhs=vb16[:n, h * 64:(h + 1) * 64],
                                               start=True, stop=False), reads=["at16", "vb16"], writes=["ps_pb"])
            if sample:
                for sq in range(4):
                    P.op("pe", lambda e, h=h, sq=sq: e.matmul(out=pb[:n, h * 64:(h + 1) * 64], lhsT=qbTm[:, sq, h, :n], rhs=Ssbf[:, sq, h, :],
                                                              start=False, stop=(sq == 3)), reads=["qbTm", "Ssbf"], writes=["ps_pb"])
            else:
                P.op("pe", lambda e, h=h: e.matmul(out=pb[:n, h * 64:(h + 1) * 64], lhsT=qbT[:, h, :n], rhs=Sbf[:, h, :],
                                                   start=False, stop=True), reads=["qbT", "Sbf"], writes=["ps_pb"])
        if opts.get("stage", 99) < 6:
            return
        pc3 = pc[0:64, 0:512].rearrange("p (h e) -> p h e", e=64)
        if sample:
            for sq in range(4):
                for h in range(8):
                    P.op("pe", lambda e, h=h, sq=sq: e.matmul(out=pc3[:, h, :], lhsT=kUm[:n, sq, h * 64:(h + 1) * 64], rhs=vb16[:n, h * 64:(h + 1) * 64],
                                                              start=True, stop=True), reads=["kUm", "vb16"], writes=["ps_pc"])
                P.op("dve", lambda e, sq=sq: e.tensor_tensor(out=Ss[:, sq], in0=Ss[:, sq], in1=cdec8[:], op=ALU.mult), reads=["Ss", "rope"], writes=["Ss"])
                P.op("dve", lambda e, sq=sq: e.tensor_tensor(out=Ss[:, sq], in0=Ss[:, sq], in1=pc3, op=ALU.add), reads=["Ss", "ps_pc"], writes=["Ss"])
            for sq in range(4):
                P.op("sp", lambda e, sq=sq: e.dma_start(out=ret_s[sq].rearrange("h d e -> d h e"), in_=Ss[:, sq]), reads=["Ss"], dma="Ss")
        else:
            for h in range(8):
                P.op("pe", lambda e, h=h: e.matmul(out=pc3[:, h, :], lhsT=kU16[:n, h * 64:(h + 1) * 64], rhs=vb16[:n, h * 64:(h + 1) * 64],
                                                   start=True, stop=True), reads=["kU16", "vb16"], writes=["ps_pc"])
            P.op("dve", lambda e: e.tensor_tensor(out=S[:], in0=S[:], in1=cdec[:], op=ALU.mult), reads=["S", "rope"], writes=["S"])
            P.op("dve", lambda e: e.tensor_tensor(out=S[:], in0=S[:], in1=pc3, op=ALU.add), reads=["S", "ps_pc"], writes=["S"])
            P.op("act", lambda e: e.copy(out=Sbf[:], in_=S[:]), reads=["S"], writes=["Sbf"])
        if opts.get("stage", 99) < 7:
            return
        o3 = obf[:n].rearrange("p (h d) -> p h d", d=64)
        c3 = cen[:n].rearrange("p (h d) -> p h d", d=64)
        s3 = sqb[:n].rearrange("p (h d) -> p h d", d=64)
        P.op("act", lambda e: e.copy(out=obf[:n], in_=pb[:n, 0:512]), reads=["ps_pb"], writes=["obf"])
        P.op("dve", lambda e: e.tensor_reduce(out=mu[:n], in_=o3, axis=AX.X, op=ALU.add), reads=["obf"], writes=["mu"])
        P.op("dve", lambda e: e.scalar_tensor_tensor(out=c3, in0=mu[:n].unsqueeze(2).to_broadcast([n, 8, 64]), scalar=-1.0 / 64, in1=o3,
                                                     op0=ALU.mult, op1=ALU.add), reads=["mu", "obf"], writes=["cen"])
        P.op("dve", lambda e: e.tensor_tensor(out=sqb[:n], in0=cen[:n], in1=cen[:n], op=ALU.mult), reads=["cen"], writes=["sqb"])
        P.op("dve", lambda e: e.tensor_reduce(out=var[:n], in_=s3, axis=AX.X, op=ALU.add), reads=["sqb"], writes=["var"])
        P.op("act", lambda e: e.activation(out=var[:n], in_=var[:n], func=AF.Sqrt, bias=EPS, scale=1.0 / 64), reads=["var"], writes=["var"])
        P.op("dve", lambda e: e.reciprocal(out=var[:n], in_=var[:n]), reads=["var"], writes=["var"])
        P.op("dve", lambda e: e.tensor_tensor(out=c3, in0=c3, in1=var[:n].unsqueeze(2).to_broadcast([n, 8, 64]), op=ALU.mult),
             reads=["cen", "var"], writes=["cen"])
        P.op("dve", lambda e: e.tensor_tensor(out=cen[:n], in0=cen[:n], in1=gnw[:n], op=ALU.mult), reads=["cen", "gnw"], writes=["cen"])
        P.op("dve", lambda e: e.tensor_tensor(out=ob16[:n], in0=cen[:n], in1=gs[:n], op=ALU.mult), reads=["cen", "gs"], writes=["ob16"])
        for c in range(4):
            P.op("pe", lambda e, c=c: e.transpose(out=pT[:, c, :n], in_=ob16[:n, c * 128:(c + 1) * 128], identity=ident[:n, :n]),
                 reads=["ob16", "ident"], writes=["ps_pT"])
        if sample:
            P.op("act", lambda e: e.copy(out=oT_st[:, :, 0:n], in_=pT[:, 0:4, :n]), reads=["ps_pT"], writes=["oT_st"])
        else:
            P.op("act", lambda e: e.copy(out=oT_st[:, :, sub * 128:(sub + 1) * 128], in_=pT[:, 0:4, :]), reads=["ps_pT"], writes=["oT_st"])

    def flush0a(st):
        t0 = st * 512
        P.op("sp", lambda e: e.dma_start(out=qaT.rearrange("c p t -> p c t")[:, :, t0:t0 + 512], in_=qaT_st[:]), reads=["qaT_st"], writes=["qaT"], dma="qaT_st")
        P.op("sp", lambda e: e.dma_start(out=kaT.rearrange("c p t -> p c t")[:, :, t0:t0 + 512], in_=kaT_st[:]), reads=["kaT_st"], writes=["kaT"], dma="kaT_st")
        P.op("sp", lambda e: e.dma_start(out=oT[512:1024, :].rearrange("(c p) t -> p c t", p=128)[:, :, t0:t0 + 512], in_=oT_st[:]),
             reads=["oT_st"], writes=["oT"], dma="oT_st")
        P.op("sp", lambda e: e.dma_start(out=va16.rearrange("h p n c -> p h n c")[:, :, st * 4:(st + 1) * 4, :], in_=v_st[:]), reads=["v_st"], writes=["va16"], dma="v_st")

    for i in range(opts.get("ntiles", NT)):
        tile0a(i, 128, False)
        if i % 4 == 3 and opts.get("flush", True):
            flush0a(i // 4)
    P.op("sp", lambda e: e.dma_start(out=ret_p.rearrange("h d e -> d h e"), in_=S[:]), reads=["S"], dma="S")
    for sq in range(4):
        P.op("sp", lambda e, sq=sq: e.dma_start(out=Ss[:, sq], in_=state_ret[sq].rearrange("h d e -> d h e")), writes=["Ss"], dma="Ss")
    P.op("act", lambda e: e.copy(out=Ssbf[:], in_=Ss[:]), reads=["Ss"], writes=["Ssbf"])
    if opts.get("sample", True):
        tile0a(NT, TS, True)
    if opts.get("sample", True) and opts.get("stage", 99) >= 8:
        P.op("sp", lambda e: e.dma_start(out=qaTs.rearrange("c p t -> p c t"), in_=qaT_st[:, :, 0:TS]), reads=["qaT_st"], writes=["qaTs"], dma="qaT_st")
        P.op("sp", lambda e: e.dma_start(out=kaTs.rearrange("c p t -> p c t"), in_=kaT_st[:, :, 0:TS]), reads=["kaT_st"], writes=["kaTs"], dma="kaT_st")
        P.op("sp", lambda e: e.dma_start(out=oT[512:1024, :].rearrange("(c p) t -> p c t", p=128)[:, :, T:T + TS], in_=oT_st[:, :, 0:TS]),
             reads=["oT_st"], writes=["oT"], dma="oT_st")
        P.op("sp", lambda e: e.dma_start(out=va16s, in_=v_st[:TS, :, 0, :]), reads=["v_st"], writes=["va16s"], dma="v_st")
    P.barrier()
    sb.reset(gmark)

    NB = T // 256
    NBp = max(NB, 8)
    NQG = T // 512
    pd = nc.alloc_psum_tensor("pd", [128, 512], F32)
    indc = sb.alloc("indc", [NBp, T], BF16)
    P.op("pool", lambda e: e.dma_start(out=indc[:], in_=cin["indc"]), writes=["indc"], dma="indc")
    Mpast = sb.alloc("Mpast", [128, NT, NBp], F32)
    OwnM = sb.alloc("OwnM", [128, NT, NBp], F32)
    P.op("sp", lambda e: e.dma_start(out=Mpast[:], in_=cin["Mpast"]), writes=["Mpast"], dma="Mpast")
    P.op("sp", lambda e: e.dma_start(out=OwnM[:], in_=cin["OwnM"]), writes=["OwnM"], dma="OwnM")
    KT = sb.alloc("KT", [64, T], BF16)
    QT = sb.alloc("QT", [64, T], BF16)
    Vh = sb.alloc("Vh", [128, NT, 65], BF16)
    kmf = sb.alloc("kmf", [64, NBp], F32)
    km16 = sb.alloc("km16", [64, NBp], BF16)
    sbk = sb.alloc("sbk", [128, NT, NBp], F32)
    t8 = sb.alloc("t8", [128, NT, 8], F32)
    thr = sb.alloc("thr", [128, NT], F32)
    bias16 = sb.alloc("bias16", [128, NT, NBp], BF16)
    biasT = sb.alloc("biasT", [NBp, T], BF16)
    p16 = [sb.alloc("p16", [128, 512], BF16) for _ in range(2)]
    rec = sb.alloc("rec", [128, 4], F32)
    o16 = sb.alloc("o16", [128, 4, 64], BF16)
    oTh = sb.alloc("oTh", [64, 512], BF16)
    Obank = [(pa, 0, "ps_pa0"), (pa, 512, "ps_pa1"), (pb, 0, "ps_pb0"), (pb, 512, "ps_pb1")]
    Sbank = [(pc, 0, "ps_pc0"), (pc, 512, "ps_pc1")]
    P.op("dve", lambda e: e.memset(kmf[:], 0.0), writes=["kmf"])

    def moba_prompt_head(h):
        hp, r0 = h // 2, (h % 2) * 64
        P.op("sp", lambda e: e.dma_start(out=KT[:], in_=kaT[hp, r0:r0 + 64, :]), reads=["kaT"], writes=["KT"], dma="KT")
        P.op("sp", lambda e: e.dma_start(out=QT[:], in_=qaT[hp, r0:r0 + 64, :]), reads=["qaT"], writes=["QT"], dma="QT")
        P.op("sp", lambda e: e.dma_start(out=Vh[:], in_=va16[h]), reads=["va16"], writes=["Vh"], dma="Vh")
        P.op("dve", lambda e: e.tensor_reduce(out=kmf[:, 0:NB], in_=KT[:].rearrange("p (n k) -> p n k", k=256), axis=AX.X, op=ALU.add),
             reads=["KT"], writes=["kmf"])
        P.op("dve", lambda e: e.tensor_copy(out=km16[:], in_=kmf[:]), reads=["kmf"], writes=["km16"])
        for g0 in range(0, NT, 16):
            g1 = min(NT, g0 + 16)
            pdv = pd[:, 0:(g1 - g0) * NBp].rearrange("p (t n) -> p t n", n=NBp)
            for t in range(g0, g1):
                P.op("pe", lambda e, t=t, g0=g0, pdv=pdv: e.matmul(out=pdv[:, t - g0, :], lhsT=QT[:, t * 128:(t + 1) * 128], rhs=km16[:], start=True, stop=True),
                     reads=["QT", "km16"], writes=["ps_pd"])
            P.op("dve", lambda e, g0=g0, g1=g1, pdv=pdv: e.tensor_tensor(out=sbk[:, g0:g1, :], in0=pdv, in1=Mpast[:, g0:g1, :], op=ALU.add),
                 reads=["ps_pd", "Mpast"], writes=["sbk"])
        for t in range(NT):
            P.op("dve", lambda e, t=t: e.max(out=t8[:, t, :], in_=sbk[:, t, :]), reads=["sbk"], writes=["t8"])
        P.op("dve", lambda e: e.tensor_scalar(out=thr[:], in0=t8[:, :, 2], scalar1=-5000.0, scalar2=None, op0=ALU.max), reads=["t8"], writes=["thr"])
        P.op("dve", lambda e: e.tensor_tensor(out=sbk[:], in0=sbk[:], in1=thr[:].unsqueeze(2).to_broadcast([128, NT, NBp]), op=ALU.is_ge),
             reads=["sbk", "thr"], writes=["sbk"])
        P.op("dve", lambda e: e.scalar_tensor_tensor(out=bias16[:], in0=sbk[:], scalar=BIGB, in1=OwnM[:], op0=ALU.mult, op1=ALU.add),
             reads=["sbk", "OwnM"], writes=["bias16"])
        for g0 in range(0, NT, 8):
            g1 = min(NT, g0 + 8)
            for t in range(g0, g1):
                P.op("pe", lambda e, t=t, g0=g0: e.transpose(out=pT[0:NBp, t - g0, :], in_=bias16[:, t, :], identity=ident[:]),
                     reads=["bias16", "ident"], writes=["ps_pT"])
            P.op("act", lambda e, g0=g0, g1=g1: e.copy(out=biasT[:, g0 * 128:g1 * 128].rearrange("p (t q) -> p t q", q=128), in_=pT[0:NBp, 0:g1 - g0, :]),
                 reads=["ps_pT"], writes=["biasT"])
        it = 0
        for qg in range(NQG):
            q0 = qg * 512
            nkc = 4 * qg + 4
            for kc in range(nkc):
                d = kc - 4 * qg
                c0 = max(d, 0) * 128
                sbt, so, skey = Sbank[it % 2]
                pt16, pkey = p16[it % 2], "p16_%d" % (it % 2)
                it += 1
                P.op("pe", lambda e, kc=kc, c0=c0, sbt=sbt, so=so, q0=q0: e.matmul(out=sbt[:, so + c0:so + 512], lhsT=KT[:, kc * 128:(kc + 1) * 128],
                                                                           rhs=QT[:, q0 + c0:q0 + 512], start=True, stop=False),
                     reads=["KT", "QT"], writes=[skey])
                P.op("pe", lambda e, kc=kc, c0=c0, sbt=sbt, so=so, q0=q0: e.matmul(out=sbt[:, so + c0:so + 512], lhsT=indc[:, kc * 128:(kc + 1) * 128],
                                                                           rhs=biasT[:, q0 + c0:q0 + 512], start=False, stop=True),
                     reads=["indc", "biasT"], writes=[skey])
                P.op("act", lambda e, c0=c0, sbt=sbt, so=so, pt16=pt16: e.activation(out=pt16[:, c0:512], in_=sbt[:, so + c0:so + 512], func=AF.Exp, scale=0.125),
                     reads=[skey], writes=[pkey])
                if d >= 0:
                    P.op("dve", lambda e, c0=c0, pt16=pt16: e.tensor_tensor(out=pt16[:, c0:c0 + 128], in0=pt16[:, c0:c0 + 128], in1=tri[:], op=ALU.mult),
                         reads=[pkey, "tri"], writes=[pkey])
                for j in range(max(d, 0), 4):
                    ob, oo, okey = Obank[j]
                    P.op("pe", lambda e, j=j, kc=kc, ob=ob, oo=oo, pt16=pt16, st_=(kc == 0), sp_=(kc == 4 * qg + j): e.matmul(
                        out=ob[:, oo:oo + 65], lhsT=pt16[:, j * 128:(j + 1) * 128], rhs=Vh[:, kc, :], start=st_, stop=sp_),
                         reads=[pkey, "Vh"], writes=[okey])
                    if kc == 4 * qg + j:
                        P.op("dve", lambda e, j=j, ob=ob, oo=oo: e.reciprocal(out=rec[:, j:j + 1], in_=ob[:, oo + 64:oo + 65]), reads=[okey], writes=["rec"])
                        P.op("dve", lambda e, j=j, ob=ob, oo=oo: e.tensor_scalar(out=o16[:, j, :], in0=ob[:, oo:oo + 64], scalar1=rec[:, j:j + 1], scalar2=None, op0=ALU.mult),
                             reads=[okey, "rec"], writes=["o16"])
            for j in range(4):
                P.op("pe", lambda e, j=j: e.transpose(out=pT[0:64, j, :], in_=o16[:, j, :], identity=ident[:]), reads=["o16", "ident"], writes=["ps_pT"])
            P.op("act", lambda e: e.copy(out=oTh[:].rearrange("p (j q) -> p j q", q=128), in_=pT[0:64, 0:4, :]), reads=["ps_pT"], writes=["oTh"])
            P.op("sp", lambda e, q0=q0: e.dma_start(out=oT[h * 64:(h + 1) * 64, q0:q0 + 512], in_=oTh[:]), reads=["oTh"], writes=["oT"], dma="oTh")

    if opts.get("moba_p", True):
        for h in range(opts.get("nheads", 8)):
            moba_prompt_head(h)
    P.barrier()
    sb.reset(gmark)

    NBs = NPG // 2
    NBsp = max(NBs, 8)
    ptb = sb.alloc("ptb", [128, NPG], I32)
    idxall = sb.alloc("idxall", [128, NPG // 2], I32)
    idxf = sb.alloc("idxf", [128, NPG // 2], F32)
    gcon = sb.alloc("gcon", [128, 3], F32)
    P.op("sp", lambda e: e.dma_start(out=gcon[:], in_=cin["gcon"]), writes=["gcon"], dma="gcon")
    iotap = sb.alloc("iotap", [128, 1], F32)
    ptf = sb.alloc("ptf", [128, NPG], F32)
    P.op("sp", lambda e: e.dma_start(out=iotap[:], in_=cin_i["iota_p"]), writes=["iotap"], dma="iotap")
    ones_f = sb.alloc("ones_f", [128, 64], F32)
    P.op("dve", lambda e: e.memset(ones_f[:], 1.0), writes=["ones_f"])
    QTs = sb.alloc("QTs", [128, 4, TS], BF16)
    KTs = sb.alloc("KTs", [128, 4, TS], BF16)
    P.op("sp", lambda e: e.dma_start(out=QTs[:], in_=qaTs.rearrange("c p t -> p c t")), reads=["qaTs"], writes=["QTs"], dma="QTs")
    P.op("sp", lambda e: e.dma_start(out=KTs[:], in_=kaTs.rearrange("c p t -> p c t")), reads=["kaTs"], writes=["KTs"], dma="KTs")
    Kb2 = [sb.alloc("Kb2", [128, 2, 512], F32) for _ in range(2)]
    Vb2 = [sb.alloc("Vb2", [128, 2, 512], F32) for _ in range(2)]
    K16 = sb.alloc("K16", [128, 2, 512], BF16)
    V16 = sb.alloc("V16", [128, 2, 8, 65], BF16)
    P.op("pool", lambda e: e.memset(V16[:], 1.0), writes=["V16"])
    KT16 = sb.alloc("KT16", [128, 8, 128], BF16)
    Sf = sb.alloc("Sf", [128, 2, 64], F32)
    P16 = sb.alloc("P16", [128, 2, 64], BF16)
    Nall = sb.alloc("Nall", [65, NBs + 1, 64], F32)
    sblk = sb.alloc("sblk", [64, NBsp], F32)
    st8 = sb.alloc("st8", [64, 8], F32)
    sthr = sb.alloc("sthr", [64, 1], F32)
    sel = sb.alloc("sel", [64, NBs], F32)
    selT = sb.alloc("selT", [NBs, 64], F32)
    selbc = sb.alloc("selbc", [65, NBs + 1, 64], F32)
    prod = sb.alloc("prod", [65, NBs + 1, 64], F32)
    resT = sb.alloc("resT", [65, 64], F32)
    rrow = sb.alloc("rrow", [65, 64], F32)
    Ot16 = sb.alloc("Ot16", [64, 64], BF16)
    rtok = sb.alloc("rtok", [64, 65], F32)
    otok = sb.alloc("otok", [64, 64], BF16)
    ownmask = sb.alloc("ownmask", [128, 64], BF16)
    P.op("pool", lambda e: e.dma_start(out=ownmask[:], in_=cin["ownmask"]), writes=["ownmask"], dma="ownmask")
    identf = sb.alloc("identf", [128, 128], F32)
    P.op("sp", lambda e: e.dma_start(out=identf[:], in_=cin["ident"]), writes=["identf"], dma="identf")
    pTf = pd

    NallT = sb.alloc("NallT", [64, NBs + 1, 65], F32)
    prodT = sb.alloc("prodT", [64, NBs + 1, 65], F32)
    dtmp = sb.alloc("dtmp", [64, 8, 65], F32)
    dmask = sb.alloc("dmask", [64, 8, 65], F32)
    P.op("sp", lambda e: e.dma_start(out=dmask[:], in_=cin["dmask"]), writes=["dmask"], dma="dmask")

    def pv_block(n_, npg):
        for half, (bk, bo, bkey) in enumerate(((pb, 0, "ps_pb0"), (pb, 512, "ps_pb1"))):
            for pg in range(npg):
                P.op("pe", lambda e, pg=pg, half=half, bk=bk, bo=bo: e.matmul(
                    out=bk[0:64, bo:bo + 260], lhsT=P16[:, pg, :], rhs=V16[:, pg, half * 4:half * 4 + 4, :].rearrange("p h d -> p (h d)"),
                    start=(pg == 0), stop=(pg == npg - 1)), reads=["P16", "V16"], writes=[bkey])
            P.op("dve", lambda e, half=half, bk=bk, bo=bo: e.tensor_tensor(
                out=dtmp[:, half * 4:half * 4 + 4, :], in0=bk[0:64, bo:bo + 260].rearrange("p (h d) -> p h d", d=65),
                in1=dmask[:, half * 4:half * 4 + 4, :], op=ALU.mult), reads=[bkey, "dmask"], writes=["dtmp"])
        P.op("dve", lambda e, n_=n_: e.tensor_reduce(out=NallT[:, n_, :], in_=dtmp[:].rearrange("p h d -> p d h"), axis=AX.X, op=ALU.add),
             reads=["dtmp"], writes=["NallT"])

    def moba_sample_seq(b):
        P.op("sp", lambda e: e.dma_start(out=ptb[:], in_=page_table[b].partition_broadcast(128)), writes=["ptb"], dma="ptb")
        P.op("dve", lambda e: e.tensor_copy(out=ptf[:], in_=ptb[:]), reads=["ptb"], writes=["ptf"])
        pv_ = ptf[:].rearrange("p (n two) -> p n two", two=2)
        P.op("dve", lambda e: e.tensor_scalar(out=idxf[:], in0=pv_[:, :, 0], scalar1=gcon[:, 0:1], scalar2=None, op0=ALU.mult), reads=["ptf", "gcon"], writes=["idxf"])
        P.op("dve", lambda e: e.scalar_tensor_tensor(out=idxf[:], in0=pv_[:, :, 1], scalar=gcon[:, 1:2], in1=idxf[:], op0=ALU.mult, op1=ALU.add),
             reads=["ptf", "gcon", "idxf"], writes=["idxf"])
        P.op("dve", lambda e: e.tensor_scalar(out=idxf[:], in0=idxf[:], scalar1=gcon[:, 2:3], scalar2=None, op0=ALU.add), reads=["idxf", "gcon"], writes=["idxf"])
        P.op("dve", lambda e: e.tensor_copy(out=idxall[:], in_=idxf[:]), reads=["idxf"], writes=["idxall"])
        P.op("dve", lambda e: e.memset(sblk[:], -1.0e4), writes=["sblk"])
        pblk = pc[0:64, 512:512 + NBs]
        for n_ in range(NBs):
            s = n_ % 2
            kb_, vb_ = Kb2[s], Vb2[s]
            P.op("pool", lambda e, n_=n_, kb_=kb_: e.indirect_dma_start(out=kb_[:].rearrange("p t f -> p (t f)"), out_offset=None, in_=cache_k2,
                                                                    in_offset=bass.IndirectOffsetOnAxis(ap=idxall[:, n_:n_ + 1], axis=0)),
                 reads=["idxall"], writes=["Kb2%d" % s], dma="Kb2%d" % s)
            P.op("pool", lambda e, n_=n_, vb_=vb_: e.indirect_dma_start(out=vb_[:].rearrange("p t f -> p (t f)"), out_offset=None, in_=cache_v2,
                                                                    in_offset=bass.IndirectOffsetOnAxis(ap=idxall[:, n_:n_ + 1], axis=0)),
                 reads=["idxall"], writes=["Vb2%d" % s], dma="Vb2%d" % s)
            P.op("dve", lambda e, kb_=kb_: e.tensor_copy(out=K16[:], in_=kb_[:]), reads=["Kb2%d" % s], writes=["K16"])
            P.op("dve", lambda e, vb_=vb_: e.tensor_copy(out=V16[:, :, :, 0:64], in_=vb_[:].rearrange("p g (h d) -> p g h d", d=64)),
                 reads=["Vb2%d" % s], writes=["V16"])
            for pg in range(2):
                for c in range(4):
                    P.op("pe", lambda e, pg=pg, c=c: e.transpose(out=pT[:, pg * 4 + c, :], in_=K16[:, pg, c * 128:(c + 1) * 128], identity=ident[:]),
                         reads=["K16", "ident"], writes=["ps_pT"])
            P.op("act", lambda e: e.copy(out=KT16[:], in_=pT[:]), reads=["ps_pT"], writes=["KT16"])
            ps_s = pa[:, 0:128].rearrange("p (g c) -> p g c", c=64)
            for pg in range(2):
                for h in range(8):
                    r0 = (h % 2) * 64
                    P.op("pe", lambda e, pg=pg, h=h, r0=r0: e.matmul(out=ps_s[:, pg, h * 8:(h + 1) * 8], lhsT=KT16[r0:r0 + 64, pg * 4 + h // 2, :],
                                                                    rhs=QTs[r0:r0 + 64, h // 2, b * 8:(b + 1) * 8], start=True, stop=True),
                         reads=["KT16", "QTs"], writes=["ps_pa0"])
            P.op("dve", lambda e, ps_s=ps_s: e.tensor_copy(out=Sf[:], in_=ps_s), reads=["ps_pa0"], writes=["Sf"])
            P.op("act", lambda e, ps_s=ps_s: e.activation(out=P16[:], in_=ps_s, func=AF.Exp, scale=0.125), reads=["ps_pa0"], writes=["P16"])
            for pg in range(2):
                P.op("pe", lambda e, pg=pg, n_=n_: e.matmul(out=pblk[:, n_:n_ + 1], lhsT=Sf[:, pg, :], rhs=ones_f[:, 0:1], start=(pg == 0), stop=(pg == 1)),
                     reads=["Sf", "ones_f"], writes=["ps_pc1"])
            pv_block(n_, 2)
        kb0, vb0 = Kb2[0], Vb2[0]
        P.op("dve", lambda e: e.memset(kb0[:, 0, :], 0.0), writes=["Kb20"])
        P.op("dve", lambda e: e.memset(vb0[:, 0, :], 0.0), writes=["Vb20"])
        P.op("sp", lambda e: e.dma_start(out=kb0[0:8, 0, :], in_=ka_s[b * 8:(b + 1) * 8, :]), reads=["ka_s_out"], writes=["Kb20"], dma="Kb20")
        P.op("sp", lambda e: e.dma_start(out=vb0[0:8, 0, :], in_=va_s[b * 8:(b + 1) * 8, :]), reads=["va_s_out"], writes=["Vb20"], dma="Vb20")
        P.op("dve", lambda e: e.tensor_copy(out=K16[:, 0, :], in_=kb0[:, 0, :]), reads=["Kb20"], writes=["K16"])
        P.op("dve", lambda e: e.tensor_copy(out=V16[:, 0, :, 0:64], in_=vb0[:, 0, :].rearrange("p (h d) -> p h d", d=64)), reads=["Vb20"], writes=["V16"])
        for c in range(4):
            P.op("pe", lambda e, c=c: e.transpose(out=pT[:, c, :], in_=K16[:, 0, c * 128:(c + 1) * 128], identity=ident[:]),
                 reads=["K16", "ident"], writes=["ps_pT"])
        P.op("act", lambda e: e.copy(out=KT16[:, 0:4, :], in_=pT[:, 0:4, :]), reads=["ps_pT"], writes=["KT16"])
        for h in range(8):
            r0 = (h % 2) * 64
            P.op("pe", lambda e, h=h, r0=r0: e.matmul(out=pa[:, h * 8:(h + 1) * 8], lhsT=KT16[r0:r0 + 64, h // 2, :],
                                                     rhs=QTs[r0:r0 + 64, h // 2, b * 8:(b + 1) * 8], start=True, stop=True),
                 reads=["KT16", "QTs"], writes=["ps_pa0"])
        P.op("act", lambda e: e.activation(out=P16[:, 0, :], in_=pa[:, 0:64], func=AF.Exp, scale=0.125), reads=["ps_pa0"], writes=["P16"])
        P.op("dve", lambda e: e.tensor_tensor(out=P16[:, 0, :], in0=P16[:, 0, :], in1=ownmask[:], op=ALU.mult), reads=["P16", "ownmask"], writes=["P16"])
        pv_block(NBs, 1)
        if opts.get("ms_stage", 99) < 3:
            return
        P.op("dve", lambda e: e.tensor_copy(out=sblk[:, 0:NBs], in_=pblk), reads=["ps_pc1"], writes=["sblk"])
        P.op("dve", lambda e: e.max(out=st8[:], in_=sblk[:]), reads=["sblk"], writes=["st8"])
        P.op("dve", lambda e: e.tensor_scalar(out=sel[:], in0=sblk[:, 0:NBs], scalar1=st8[:, 2:3], scalar2=None, op0=ALU.is_ge), reads=["sblk", "st8"], writes=["sel"])
        P.op("dve", lambda e: e.tensor_tensor(out=prodT[:, 0:NBs, :], in0=NallT[:, 0:NBs, :], in1=sel[:].unsqueeze(2).to_broadcast([64, NBs, 65]), op=ALU.mult),
             reads=["NallT", "sel"], writes=["prodT"])
        P.op("dve", lambda e: e.tensor_copy(out=prodT[:, NBs, :], in_=NallT[:, NBs, :]), reads=["NallT"], writes=["prodT"])
        P.op("dve", lambda e: e.tensor_reduce(out=rtok[:], in_=prodT[:].rearrange("p n d -> p d n"), axis=AX.X, op=ALU.add), reads=["prodT"], writes=["rtok"])
        P.op("dve", lambda e: e.reciprocal(out=rtok[:, 64:65], in_=rtok[:, 64:65]), reads=["rtok"], writes=["rtok"])
        P.op("dve", lambda e: e.tensor_scalar(out=otok[:], in0=rtok[:, 0:64], scalar1=rtok[:, 64:65], scalar2=None, op0=ALU.mult), reads=["rtok"], writes=["otok"])
        P.op("pe", lambda e: e.transpose(out=pT[0:64, 0, 0:64], in_=otok[:], identity=ident[0:64, 0:64]), reads=["otok", "ident"], writes=["ps_pT"])
        P.op("act", lambda e: e.copy(out=Ot16[:], in_=pT[0:64, 0, 0:64]), reads=["ps_pT"], writes=["Ot16"])
        if opts.get("ms_stage", 99) < 7:
            return
        P.op("sp", lambda e: e.dma_start(out=oT[0:512, T + b * 8:T + b * 8 + 8].rearrange("(h d) q -> d h q", d=64),
                                         in_=Ot16[:].rearrange("p (h q) -> p h q", q=8)), reads=["Ot16"], writes=["oT"], dma="Ot16")

    if opts.get("moba_s", True):
        for b in opts.get("seqs", range(opts.get("nseq", 4))):
            moba_sample_seq(b)
    P.barrier()
    sb.reset(gmark)

    def phase_c(l, oTs, w_out_d, xsrc_p, xsrc_s, dst_p, dst_s):
        m0 = sb.mark()
        Wo = sb.alloc("Wo", [128, 8, D], BF16)
        Wg = sb.alloc("Wg", [128, 8, DFF], BF16)
        Wu = sb.alloc("Wu", [128, 8, DFF], BF16)
        Wd = sb.alloc("Wd", [128, 22, D], BF16)
        load_w_bf16(Wo, w_out_d, 8, D, "Wo")
        load_w_bf16(Wg, ffn_w_gate[l], 8, DFF, "Wg")
        load_w_bf16(Wu, ffn_w_up[l], 8, DFF, "Wu")
        vd = ffn_w_down[l].rearrange("(k p) n -> p k n", p=128)
        for k0 in (0, 11):
            for c0 in (0, 512):
                P.op("pool", lambda e, k0=k0, c0=c0: e.dma_start(out=Wd[:, k0:k0 + 11, c0:c0 + 512], in_=vd[:, k0:k0 + 11, c0:c0 + 512]), writes=["Wd"], dma="Wd")
        nfb = sb.alloc("nfb", [128, D], F32)
        P.op("sp", lambda e: e.dma_start(out=nfb[:], in_=norm_ffn[l].partition_broadcast(128)), writes=["nfb"], dma="nfb")
        if l == 1:
            nfin = sb.alloc("nfin", [128, D], F32)
            P.op("sp", lambda e: e.dma_start(out=nfin[:], in_=norm_final.partition_broadcast(128)), writes=["nfin"], dma="nfin")
            yout = sb.alloc("yout", [128, D], F32)
        oTt = sb.alloc("oTt", [128, 8, 256], BF16)
        xt2 = sb.alloc("xt2", [128, 2, D], F32)
        hT2 = sb.alloc("hT2", [128, 8, 256], BF16)
        HT = sb.alloc("HT", [128, 22, 256], BF16)
        gt = sb.alloc("gt", [128, 256], F32)
        k.junk = sb.alloc("junkc", [128, D], BF16)
        k.ssq = [sb.alloc("ssqc", [128, 1], F32) for _ in range(2)]
        k.rstd = [sb.alloc("rstdc", [128, 1], F32) for _ in range(2)]
        k.h16 = [sb.alloc("h16c", [128, D], BF16) for _ in range(2)]
        ybank = [pa, pb]
        ykey = ["ps_pa", "ps_pb"]

        def norm_to(xv, n, wb, wkey, u, outT, col0):
            ssq, rstd, junk, h16 = k.ssq[u], k.rstd[u], k.junk, k.h16[u]
            P.op("act", lambda e: e.activation(out=junk[:n], in_=xv, func=AF.Square, accum_out=ssq[:n]), reads=["xt2"], writes=["junk", "ssq%d" % u])
            P.op("act", lambda e: e.activation(out=rstd[:n], in_=ssq[:n], func=AF.Sqrt, bias=EPS, scale=1.0 / D), reads=["ssq%d" % u], writes=["rstd%d" % u])
            P.op("dve", lambda e: e.reciprocal(out=rstd[:n], in_=rstd[:n]), reads=["rstd%d" % u], writes=["rstd%d" % u])
            return ssq, rstd, h16

        def ctile(t0, ntok, xsrc, dst, sample):
            nu = 1 if sample else 2
            n = ntok if sample else 128
            P.op("sp", lambda e: e.dma_start(out=oTt[:, :, 0:ntok], in_=oTs[:, t0:t0 + ntok].rearrange("(c p) t -> p c t", p=128)),
                 reads=["oT"], writes=["oTt"], dma="oTt")
            if sample:
                P.op("sp", lambda e: e.dma_start(out=xt2[:n, 0, :], in_=xsrc), reads=["x1scr"], writes=["xt2"], dma="xt2")
            else:
                P.op("sp", lambda e: e.dma_start(out=xt2[:], in_=xsrc.rearrange("(u p) d -> p u d", p=128)), reads=["x1scr"], writes=["xt2"], dma="xt2")
            for u in range(nu):
                yb, yk = ybank[u], ykey[u]
                for g in range(2):
                    for c in range(8):
                        P.op("pe", lambda e, u=u, g=g, c=c, yb=yb: e.matmul(out=yb[:n, g * 512:(g + 1) * 512], lhsT=oTt[:, c, u * 128:u * 128 + n],
                                                                           rhs=Wo[:, c, g * 512:(g + 1) * 512], start=(c == 0), stop=(c == 7)),
                             reads=["oTt", "Wo"], writes=[yk])
                P.op("dve", lambda e, u=u, yb=yb: e.tensor_tensor(out=xt2[:n, u, :], in0=xt2[:n, u, :], in1=yb[:n, :], op=ALU.add), reads=["xt2", yk], writes=["xt2"])
                ssq, rstd, h16 = norm_to(xt2[:n, u, :], n, nfb, "nfb", u, hT2, u * 128)
                P.op("dve", lambda e, u=u, rstd=rstd, h16=h16: e.scalar_tensor_tensor(out=h16[:n], in0=xt2[:n, u, :], scalar=rstd[:n, 0:1], in1=nfb[:n],
                                                                                     op0=ALU.mult, op1=ALU.mult), reads=["xt2", "rstd%d" % u, "nfb"], writes=["h16%d" % u])
                for c in range(8):
                    P.op("pe", lambda e, c=c, h16=h16: e.transpose(out=pT[:, c, :n], in_=h16[:n, c * 128:(c + 1) * 128], identity=ident[:n, :n]),
                         reads=["h16%d" % u, "ident"], writes=["ps_pT"])
                P.op("act", lambda e, u=u: e.copy(out=hT2[:, :, u * 128:u * 128 + n], in_=pT[:, :, :n]), reads=["ps_pT"], writes=["hT2"])
            for f in range(22):
                gb_, gk = (pc, "ps_pc0") if f % 2 == 0 else (pc, "ps_pc1")
                go = 0 if f % 2 == 0 else 512
                for (W_, wk, off) in ((Wg, "Wg", 0), (Wu, "Wu", 256)):
                    for c in range(8):
                        P.op("pe", lambda e, f=f, c=c, W_=W_, off=off, go=go: e.matmul(out=pc[:, go + off:go + off + ntok], lhsT=W_[:, c, f * 128:(f + 1) * 128],
                                                                                     rhs=hT2[:, c, 0:ntok], start=(c == 0), stop=(c == 7)),
                             reads=["hT2", wk], writes=[gk])
                P.op("act", lambda e, go=go: e.activation(out=gt[:, 0:ntok], in_=pc[:, go:go + ntok], func=AF.Silu), reads=[gk], writes=["gt"])
                P.op("dve", lambda e, f=f, go=go: e.tensor_tensor(out=HT[:, f, 0:ntok], in0=gt[:, 0:ntok], in1=pc[:, go + 256:go + 256 + ntok], op=ALU.mult),
                     reads=["gt", gk], writes=["HT"])
            for u in range(nu):
                yb, yk = ybank[u], ykey[u]
                for g in range(2):
                    for f in range(22):
                        P.op("pe", lambda e, u=u, g=g, f=f, yb=yb: e.matmul(out=yb[:n, g * 512:(g + 1) * 512], lhsT=HT[:, f, u * 128:u * 128 + n],
                                                                           rhs=Wd[:, f, g * 512:(g + 1) * 512], start=(f == 0), stop=(f == 21)),
                             reads=["HT", "Wd"], writes=[yk])
                P.op("dve", lambda e, u=u, yb=yb: e.tensor_tensor(out=xt2[:n, u, :], in0=xt2[:n, u, :], in1=yb[:n, :], op=ALU.add), reads=["xt2", yk], writes=["xt2"])
                if l == 1:
                    ssq, rstd, h16 = norm_to(xt2[:n, u, :], n, nfin, "nfin", u, None, 0)
                    P.op("dve", lambda e, u=u, rstd=rstd: e.scalar_tensor_tensor(out=yout[:n], in0=xt2[:n, u, :], scalar=rstd[:n, 0:1], in1=nfin[:n],
                                                                               op0=ALU.mult, op1=ALU.mult), reads=["xt2", "rstd%d" % u, "nfin"], writes=["yout"])
                    P.op("sp", lambda e, u=u: e.dma_start(out=dst[u * 128:u * 128 + n, :], in_=yout[:n]), reads=["yout"], writes=["ydst"], dma="yout")
            if l == 0:
                if sample:
                    P.op("sp", lambda e: e.dma_start(out=dst, in_=xt2[:n, 0, :]), reads=["xt2"], writes=["x1scr"], dma="xt2")
                else:
                    P.op("sp", lambda e: e.dma_start(out=dst.rearrange("(u p) d -> p u d", p=128), in_=xt2[:]), reads=["xt2"], writes=["x1scr"], dma="xt2")

        nst = opts.get("nst_c", T // 256)
        for st in range(nst):
            t0 = st * 256
            ctile(t0, 256, xsrc_p[t0:t0 + 256, :], dst_p[t0:t0 + 256, :], False)
        if opts.get("sample_c", True):
            ctile(T, TS, xsrc_s, dst_s, True)
        P.barrier()
        sb.reset(m0)

    if opts.get("phase_c0", True):
        phase_c(0, oT, w_out_ab, xp, xs, x1scr[0:T, :], x1scr[T:T + TS, :])

    NH1 = 16
    m1 = sb.mark()
    w_c = sb.alloc("w_c", [128, 8, 3072], BF16)
    load_w_bf16(w_c, w_in_c, 8, 3072, "w_c")
    nwb1 = sb.alloc("nwb1", [128, D], F32)
    P.op("sp", lambda e: e.dma_start(out=nwb1[:], in_=norm_mix[1].partition_broadcast(128)), writes=["nwb1"], dma="nwb1")
    cosA1 = sb.alloc("cosA1", [128, NT, 8], F32)
    sinA1 = sb.alloc("sinA1", [128, NT, 8], F32)
    cosAs1 = sb.alloc("cosAs1", [32, 1, 8], F32)
    sinAs1 = sb.alloc("sinAs1", [32, 1, 8], F32)
    for nm, t in (("cosA", cosA1), ("sinA", sinA1), ("cosAs", cosAs1), ("sinAs", sinAs1)):
        P.op("sp", lambda e, t=t, nm=nm: e.dma_start(out=t[:], in_=cin[nm]), writes=["rope"], dma="rope")
    k.junk = sb.alloc("junk1", [128, D], BF16)
    k.ssq = [sb.alloc("ssq1", [128, 1], F32) for _ in range(2)]
    k.rstd = [sb.alloc("rstd1", [128, 1], F32) for _ in range(2)]
    k.h16 = [sb.alloc("h161", [128, D], BF16) for _ in range(2)]
    k.hT = [sb.alloc("hT1", [128, 8, 128], BF16) for _ in range(2)]
    xts1 = [sb.alloc("xt1", [128, D], F32) for _ in range(2)]
    qf = sb.alloc("qf", [128, 1024], F32)
    kf = sb.alloc("kf", [128, 1024], F32)
    vf = sb.alloc("vf", [128, 1024], F32)
    x16 = sb.alloc("x16", [128, 1024], BF16)
    ropetmp1 = sb.alloc("ropetmp1", [128, 512], F32)
    qcT_st = sb.alloc("qcT_st", [128, 8, 512], BF16)
    kcT_st = sb.alloc("kcT_st", [128, 8, 512], BF16)
    v_st1 = sb.alloc("v_st1", [128, 16, 4, 65], BF16)
    P.op("dve", lambda e: e.memset(v_st1[:], 1.0), writes=["v_st1"])

    def tile1a(i, n, sample):
        s = i % 2
        xt = xts1[s]
        sub = i % 4
        src = x1scr[T:T + TS, :] if sample else x1scr[i * 128:(i + 1) * 128, :]
        P.op("sp", lambda e: e.dma_start(out=xt[:n], in_=src), reads=["x1scr"], writes=["xt%d" % s], dma="xt%d" % s)
        hT, hkey = rmsnorm_hT(xt, n, nwb1, s, "nwb1")
        if sample:
            cA, sA = cosAs1[:n, 0, :], sinAs1[:n, 0, :]
        else:
            cA, sA = cosA1[:n, i, :], sinA1[:n, i, :]
        cs = slice(0, n) if sample else slice(sub * 128, (sub + 1) * 128)
        proj(pa, "ps_pa", hT, hkey, n, w_c, "w_c", 0, 1024)
        rope(pa, "ps_pa", qf, "qf", n, 16, 8, cA, sA, ropetmp1)
        P.op("act", lambda e: e.copy(out=x16[:n], in_=qf[:n]), reads=["qf"], writes=["x16"])
        for c in range(8):
            P.op("pe", lambda e, c=c: e.transpose(out=pT[:, c, :n], in_=x16[:n, c * 128:(c + 1) * 128], identity=ident[:n, :n]),
                 reads=["x16", "ident"], writes=["ps_pT"])
        P.op("act", lambda e: e.copy(out=qcT_st[:, :, cs], in_=pT[:, :, :n]), reads=["ps_pT"], writes=["qcT_st"])
        proj(pb, "ps_pb", hT, hkey, n, w_c, "w_c", 1024, 1024)
        rope(pb, "ps_pb", kf, "kf", n, 16, 8, cA, sA, ropetmp1)
        if sample:
            for b_ in range(4):
                P.op("sp", lambda e, b_=b_: e.dma_start(out=kc_s[b_, W_MAX - 8:W_MAX, :], in_=kf[b_ * 8:(b_ + 1) * 8]), reads=["kf"], writes=["kc_s_out"], dma="kf")
        elif (i + 1) * 128 > T - WK:
            r0 = i * 128 - (T - WK)
            P.op("sp", lambda e: e.dma_start(out=kc_p[r0:r0 + 128, :], in_=kf[:n]), reads=["kf"], dma="kf")
        P.op("act", lambda e: e.copy(out=x16[:n], in_=kf[:n]), reads=["kf"], writes=["x16"])
        for c in range(8):
            P.op("pe", lambda e, c=c: e.transpose(out=pT[:, c, :n], in_=x16[:n, c * 128:(c + 1) * 128], identity=ident[:n, :n]),
                 reads=["x16", "ident"], writes=["ps_pT"])
        P.op("act", lambda e: e.copy(out=kcT_st[:, :, cs], in_=pT[:, :, :n]), reads=["ps_pT"], writes=["kcT_st"])
        proj(pc, "ps_pc", hT, hkey, n, w_c, "w_c", 2048, 1024)
        P.op("act", lambda e: e.copy(out=vf[:n], in_=pc[:n, :]), reads=["ps_pc"], writes=["vf"])
        if sample:
            for b_ in range(4):
                P.op("sp", lambda e, b_=b_: e.dma_start(out=vc_s[b_, W_MAX - 8:W_MAX, :], in_=vf[b_ * 8:(b_ + 1) * 8]), reads=["vf"], writes=["vc_s_out"], dma="vf")
        elif (i + 1) * 128 > T - WK:
            r0 = i * 128 - (T - WK)
            P.op("sp", lambda e: e.dma_start(out=vc_p[r0:r0 + 128, :], in_=vf[:n]), reads=["vf"], dma="vf")
        vsl = v_st1[:n, :, 0, 0:64] if sample else v_st1[:, :, sub, 0:64]
        P.op("dve", lambda e: e.tensor_copy(out=vsl, in_=pc[:n, :].rearrange("p (h d) -> p h d", d=64)), reads=["ps_pc"], writes=["v_st1"])

    def flush1a(st):
        t0 = st * 512
        P.op("sp", lambda e: e.dma_start(out=qcT.rearrange("c p t -> p c t")[:, :, t0:t0 + 512], in_=qcT_st[:]), reads=["qcT_st"], writes=["qcT"], dma="qcT_st")
        P.op("sp", lambda e: e.dma_start(out=kcT.rearrange("c p t -> p c t")[:, :, t0:t0 + 512], in_=kcT_st[:]), reads=["kcT_st"], writes=["kcT"], dma="kcT_st")
        for hh in (0, 8):
            P.op("sp", lambda e, hh=hh: e.dma_start(out=vc16.rearrange("h p n c -> p h n c")[:, hh:hh + 8, st * 4:(st + 1) * 4, :], in_=v_st1[:, hh:hh + 8]),
                 reads=["v_st1"], writes=["vc16"], dma="v_st1")

    for b in range(4 if opts.get("d2d", True) else 0):
        P.op("sp", lambda e, b=b: e.dma_start(out=kc_s[b, 0:W_MAX - 8, :], in_=cwk[b, 8:W_MAX, :]), dma="d2d")
        P.op("sp", lambda e, b=b: e.dma_start(out=vc_s[b, 0:W_MAX - 8, :], in_=cwv[b, 8:W_MAX, :]), dma="d2d")
    if opts.get("phase_1a", True):
        for i in range(NT):
            tile1a(i, 128, False)
            if i % 4 == 3:
                flush1a(i // 4)
        tile1a(NT, TS, True)
        P.op("sp", lambda e: e.dma_start(out=qcTs.rearrange("c p t -> p c t"), in_=qcT_st[:, :, 0:TS]), reads=["qcT_st"], writes=["qcTs"], dma="qcT_st")
        P.op("sp", lambda e: e.dma_start(out=kcTs.rearrange("c p t -> p c t"), in_=kcT_st[:, :, 0:TS]), reads=["kcT_st"], writes=["kcTs"], dma="kcT_st")
        P.op("sp", lambda e: e.dma_start(out=vc16s, in_=v_st1[:TS, :, 0, :]), reads=["v_st1"], writes=["vc16s"], dma="v_st1")
    P.barrier()
    sb.reset(m1)

    m1 = sb.mark()
    Mm = sb.alloc("Mm", [128, 20, 512], BF16)
    for e0 in range(0, 20, 5):
        P.op("pool", lambda e, e0=e0: e.dma_start(out=Mm[:, e0:e0 + 5, :], in_=cin["dilm"].rearrange("e p q -> p e q")[:, e0:e0 + 5, :]), writes=["Mm"], dma="Mm")
    KT1 = sb.alloc("KT1", [64, T], BF16)
    QT1 = sb.alloc("QT1", [64, T], BF16)
    Vh1 = sb.alloc("Vh1", [128, NT, 65], BF16)
    p16b = [sb.alloc("p16b", [128, 512], BF16) for _ in range(2)]
    rec1 = sb.alloc("rec1", [128, 4], F32)
    o161 = sb.alloc("o161", [128, 4, 64], BF16)
    oTh1 = sb.alloc("oTh1", [64, 512], BF16)

    def dil_prompt_head(h):
        hp, r0 = h // 2, (h % 2) * 64
        P.op("sp", lambda e: e.dma_start(out=KT1[:], in_=kcT[hp, r0:r0 + 64, :]), reads=["kcT"], writes=["KT1"], dma="KT1")
        P.op("sp", lambda e: e.dma_start(out=QT1[:], in_=qcT[hp, r0:r0 + 64, :]), reads=["qcT"], writes=["QT1"], dma="QT1")
        P.op("sp", lambda e: e.dma_start(out=Vh1[:], in_=vc16[h]), reads=["vc16"], writes=["Vh1"], dma="Vh1")
        it = 0
        for qg in range(NQG):
            q0 = qg * 512
            kc_first = max(0, 4 * qg - 16)
            for kc in range(kc_first, 4 * qg + 4):
                ee = 4 * qg - kc
                c0 = max(-ee, 0) * 128
                sbt, so, skey = Sbank[it % 2]
                pt16, pkey = p16b[it % 2], "p16b_%d" % (it % 2)
                it += 1
                P.op("pe", lambda e, kc=kc, c0=c0, sbt=sbt, so=so, q0=q0: e.matmul(out=sbt[:, so + c0:so + 512], lhsT=KT1[:, kc * 128:(kc + 1) * 128],
                                                                           rhs=QT1[:, q0 + c0:q0 + 512], start=True, stop=True),
                     reads=["KT1", "QT1"], writes=[skey])
                P.op("act", lambda e, c0=c0, sbt=sbt, so=so, pt16=pt16: e.activation(out=pt16[:, c0:512], in_=sbt[:, so + c0:so + 512], func=AF.Exp, scale=0.125),
                     reads=[skey], writes=[pkey])
                P.op("dve", lambda e, c0=c0, pt16=pt16, ee=ee: e.tensor_tensor(out=pt16[:, c0:512], in0=pt16[:, c0:512], in1=Mm[:, ee + 3, c0:512], op=ALU.mult),
                     reads=[pkey, "Mm"], writes=[pkey])
                for j in range(c0 // 128, 4):
                    ob, oo, okey = Obank[j]
                    P.op("pe", lambda e, j=j, kc=kc, ob=ob, oo=oo, pt16=pt16, st_=(kc == kc_first), sp_=(kc == 4 * qg + j): e.matmul(
                        out=ob[:, oo:oo + 65], lhsT=pt16[:, j * 128:(j + 1) * 128], rhs=Vh1[:, kc, :], start=st_, stop=sp_),
                         reads=[pkey, "Vh1"], writes=[okey])
                    if kc == 4 * qg + j:
                        P.op("dve", lambda e, j=j, ob=ob, oo=oo: e.reciprocal(out=rec1[:, j:j + 1], in_=ob[:, oo + 64:oo + 65]), reads=[okey], writes=["rec1"])
                        P.op("dve", lambda e, j=j, ob=ob, oo=oo: e.tensor_scalar(out=o161[:, j, :], in0=ob[:, oo:oo + 64], scalar1=rec1[:, j:j + 1], scalar2=None, op0=ALU.mult),
                             reads=[okey, "rec1"], writes=["o161"])
            for j in range(4):
                P.op("pe", lambda e, j=j: e.transpose(out=pT[0:64, j, :], in_=o161[:, j, :], identity=ident[:]), reads=["o161", "ident"], writes=["ps_pT"])
            P.op("act", lambda e: e.copy(out=oTh1[:].rearrange("p (j q) -> p j q", q=128), in_=pT[0:64, 0:4, :]), reads=["ps_pT"], writes=["oTh1"])
            P.op("sp", lambda e, q0=q0: e.dma_start(out=o1T[h * 64:(h + 1) * 64, q0:q0 + 512], in_=oTh1[:]), reads=["oTh1"], writes=["o1T"], dma="oTh1")

    if opts.get("dil_p", True):
        for h in range(opts.get("nheads1", 16)):
            dil_prompt_head(h)
    P.barrier()
    sb.reset(m1)

    m1 = sb.mark()
    NCH = W_MAX // 128
    Ms = sb.alloc("Ms", [128, NCH + 1, 8], BF16)
    P.op("pool", lambda e: e.dma_start(out=Ms[:], in_=cin["dilms"].rearrange("c p q -> p c q")), writes=["Ms"], dma="Ms")
    Mown = sb.alloc("Mown", [8, 8], BF16)
    P.op("pool", lambda e: e.dma_start(out=Mown[:], in_=cin["dilown"]), writes=["Mown"], dma="Mown")
    ones_f1 = sb.alloc("ones_f1", [128, 64], F32)
    P.op("dve", lambda e: e.memset(ones_f1[:], 1.0), writes=["ones_f1"])
    QTs1 = sb.alloc("QTs1", [128, 8, TS], BF16)
    KTs1 = sb.alloc("KTs1", [128, 8, TS], BF16)
    P.op("sp", lambda e: e.dma_start(out=QTs1[:], in_=qcTs.rearrange("c p t -> p c t")), reads=["qcTs"], writes=["QTs1"], dma="QTs1")
    P.op("sp", lambda e: e.dma_start(out=KTs1[:], in_=kcTs.rearrange("c p t -> p c t")), reads=["kcTs"], writes=["KTs1"], dma="KTs1")
    Kw = [sb.alloc("Kw", [128, 1024], F32) for _ in range(2)]
    Vw = [sb.alloc("Vw", [128, 1024], F32) for _ in range(2)]
    Kw16 = sb.alloc("Kw16", [128, 1024], BF16)
    KTw = sb.alloc("KTw", [128, 8, 128], BF16)
    Vall = sb.alloc("Vall", [128, NCH + 1, 16, 65], BF16)
    P.op("pool", lambda e: e.memset(Vall[:], 1.0), writes=["Vall"])
    Pall = sb.alloc("Pall", [128, NCH + 1, 128], BF16)
    Vown1 = sb.alloc("Vown1", [8, 16, 65], BF16)
    Pown1 = sb.alloc("Pown1", [8, 128], BF16)
    resT1 = sb.alloc("resT1", [65, 128], F32)
    rrow1 = sb.alloc("rrow1", [65, 128], F32)
    Ot161 = sb.alloc("Ot161", [64, 128], BF16)

    def dil_sample_seq(b):
        for ch in range(NCH + 1):
            s = ch % 2
            if ch < NCH:
                P.op("sp", lambda e, ch=ch, s=s: e.dma_start(out=Kw[s][:], in_=cwk[b, ch * 128:(ch + 1) * 128, :]), writes=["Kw%d" % s], dma="Kw%d" % s)
                P.op("sp", lambda e, ch=ch, s=s: e.dma_start(out=Vw[s][:], in_=cwv[b, ch * 128:(ch + 1) * 128, :]), writes=["Vw%d" % s], dma="Vw%d" % s)
            else:
                P.op("dve", lambda e, s=s: e.memset(Kw[s][:], 0.0), writes=["Kw%d" % s])
                P.op("dve", lambda e, s=s: e.memset(Vw[s][:], 0.0), writes=["Vw%d" % s])
                P.op("sp", lambda e, s=s: e.dma_start(out=Kw[s][0:8, :], in_=kc_s[b, W_MAX - 8:W_MAX, :]), reads=["kc_s_out"], writes=["Kw%d" % s], dma="Kw%d" % s)
                P.op("sp", lambda e, s=s: e.dma_start(out=Vw[s][0:8, :], in_=vc_s[b, W_MAX - 8:W_MAX, :]), reads=["vc_s_out"], writes=["Vw%d" % s], dma="Vw%d" % s)
            P.op("dve", lambda e, s=s: e.tensor_copy(out=Kw16[:], in_=Kw[s][:]), reads=["Kw%d" % s], writes=["Kw16"])
            P.op("dve", lambda e, s=s, ch=ch: e.tensor_copy(out=Vall[:, ch, :, 0:64], in_=Vw[s][:].rearrange("p (h d) -> p h d", d=64)),
                 reads=["Vw%d" % s], writes=["Vall"])
            for c in range(8):
                P.op("pe", lambda e, c=c: e.transpose(out=pT[:, c, :], in_=Kw16[:, c * 128:(c + 1) * 128], identity=ident[:]),
                     reads=["Kw16", "ident"], writes=["ps_pT"])
            P.op("act", lambda e: e.copy(out=KTw[:], in_=pT[:]), reads=["ps_pT"], writes=["KTw"])
            for h in range(16):
                r0 = (h % 2) * 64
                P.op("pe", lambda e, h=h, r0=r0: e.matmul(out=pa[:, h * 8:(h + 1) * 8], lhsT=KTw[r0:r0 + 64, h // 2, :],
                                                         rhs=QTs1[r0:r0 + 64, h // 2, b * 8:(b + 1) * 8], start=True, stop=True),
                     reads=["KTw", "QTs1"], writes=["ps_pa0"])
            P.op("act", lambda e, ch=ch: e.activation(out=Pall[:, ch, :], in_=pa[:, 0:128], func=AF.Exp, scale=0.125), reads=["ps_pa0"], writes=["Pall"])
            P.op("dve", lambda e, ch=ch: e.tensor_tensor(out=Pall[:, ch, :].rearrange("p (h q) -> p h q", q=8), in0=Pall[:, ch, :].rearrange("p (h q) -> p h q", q=8),
                                                        in1=Ms[:, ch, :].unsqueeze(1).to_broadcast([128, 16, 8]), op=ALU.mult), reads=["Pall", "Ms"], writes=["Pall"])
        for h in range(16):
            for ch in range(NCH + 1):
                P.op("pe", lambda e, h=h, ch=ch: e.matmul(out=pb[0:65, h * 8:(h + 1) * 8], lhsT=Vall[:, ch, h, :], rhs=Pall[:, ch, h * 8:(h + 1) * 8],
                                                         start=(ch == 0), stop=(ch == NCH)), reads=["Vall", "Pall"], writes=["ps_pb0"])
        P.op("dve", lambda e: e.tensor_copy(out=resT1[:], in_=pb[0:65, 0:128]), reads=["ps_pb0"], writes=["resT1"])
        P.op("dve", lambda e: e.reciprocal(out=rrow1[64:65, :], in_=resT1[64:65, :]), reads=["resT1"], writes=["rrow1"])
        P.op("pe", lambda e: e.matmul(out=pd[0:64, 0:128], lhsT=ones_f1[64:65, 0:64], rhs=rrow1[64:65, :], start=True, stop=True),
             reads=["ones_f1", "rrow1"], writes=["ps_pd"])
        P.op("dve", lambda e: e.tensor_tensor(out=Ot161[:], in0=resT1[0:64, :], in1=pd[0:64, 0:128], op=ALU.mult), reads=["resT1", "ps_pd"], writes=["Ot161"])
        P.op("sp", lambda e: e.dma_start(out=o1T[:, T + b * 8:T + b * 8 + 8].rearrange("(h d) q -> d h q", d=64),
                                         in_=Ot161[:].rearrange("p (h q) -> p h q", q=8)), reads=["Ot161"], writes=["o1T"], dma="Ot161")

    if opts.get("dil_s", True):
        for b in range(opts.get("nseq", 4)):
            dil_sample_seq(b)
    P.barrier()
    sb.reset(m1)

    if opts.get("phase_c1", True):
        phase_c(1, o1T, w_out_c, x1scr[0:T, :], x1scr[T:T + TS, :], y_p, y_s)

    P.finish()
    n = P.emit()
    return nc, consts, n


def core_inputs(inp, consts, c, T, NPG, NPOOL):
    b = c // 2
    m = {}
    m["xp"] = np.ascontiguousarray(inp["x_prompt"][b])
    m["xs"] = np.ascontiguousarray(inp["x_sample"][4 * c:4 * c + 4].reshape(32, D))
    for nm in ("norm_mix", "norm_ffn", "norm_final"):
        m[nm] = np.ascontiguousarray(inp[nm])
    m["w_in_ab"] = np.ascontiguousarray(inp["w_in_ab"][0])
    m["w_out_ab"] = np.ascontiguousarray(inp["w_out_ab"][0])
    m["ret_gn_w"] = np.ascontiguousarray(inp["ret_gn_w"][0])
    m["state_ret"] = np.ascontiguousarray(inp["state_ret"][0, 4 * c:4 * c + 4])
    for nm, v in consts.items():
        m["c_" + nm] = v
    m["w_in_c"] = np.ascontiguousarray(inp["w_in_c"][0])
    m["w_out_c"] = np.ascontiguousarray(inp["w_out_c"][0])
    m["cwk"] = np.ascontiguousarray(inp["cache_win_k"][0, 4 * c:4 * c + 4].reshape(4, W_MAX, D))
    m["cwv"] = np.ascontiguousarray(inp["cache_win_v"][0, 4 * c:4 * c + 4].reshape(4, W_MAX, D))
    for nm in ("ffn_w_gate", "ffn_w_up", "ffn_w_down"):
        m[nm] = np.ascontiguousarray(inp[nm])
    m["ci_iota_p"] = np.arange(128, dtype=np.float32).reshape(128, 1)
    m["cache_k2"] = inp["cache_k_a"][0].reshape(NPOOL * 64, 1024)
    m["cache_v2"] = inp["cache_v_a"][0].reshape(NPOOL * 64, 1024)
    m["page_table"] = np.ascontiguousarray(inp["page_table"][4 * c:4 * c + 4])
    return m


_CACHE = {}


def kernel(x_prompt, x_sample, cache_k_a, cache_v_a, page_table, state_ret, cache_win_k, cache_win_v,
           norm_mix, norm_ffn, norm_final, w_in_ab, w_out_ab, ret_gn_w, w_in_c, w_out_c,
           ffn_w_gate, ffn_w_up, ffn_w_down):
    inp = dict(x_prompt=x_prompt, x_sample=x_sample, cache_k_a=cache_k_a, cache_v_a=cache_v_a, page_table=page_table,
               state_ret=state_ret, cache_win_k=cache_win_k, cache_win_v=cache_win_v, norm_mix=norm_mix, norm_ffn=norm_ffn,
               norm_final=norm_final, w_in_ab=w_in_ab, w_out_ab=w_out_ab, ret_gn_w=ret_gn_w, w_in_c=w_in_c, w_out_c=w_out_c,
               ffn_w_gate=ffn_w_gate, ffn_w_up=ffn_w_up, ffn_w_down=ffn_w_down)
    inp = {k_: np.asarray(v) for k_, v in inp.items()}
    B, T, _ = inp["x_prompt"].shape
    NSEQ, NPG = inp["page_table"].shape
    NPOOL = inp["cache_k_a"].shape[1]
    ncores = 8
    assert B * 2 == ncores and NSEQ == 4 * ncores
    key = (T, NPG, NPOOL)
    if key not in _CACHE:
        _CACHE[key] = build(T, NPG, NPOOL)
    nc, consts, _ = _CACHE[key]
    in_maps = [core_inputs(inp, consts, c, T, NPG, NPOOL) for c in range(ncores)]
    res = run_bass_kernel_spmd(nc, in_maps, core_ids=list(range(ncores))).results
    WK = min(W_MAX, T)
    f32 = np.float32
    y_p = np.stack([res[2 * b]["y_p"] for b in range(B)]).astype(f32)
    y_s = np.concatenate([res[c]["y_s"].reshape(4, 8, D) for c in range(ncores)]).astype(f32)
    ka_p = np.stack([res[2 * b]["ka_p"].reshape(T, 8, 64) for b in range(B)])[None].astype(f32)
    va_p = np.stack([res[2 * b]["va_p"].reshape(T, 8, 64) for b in range(B)])[None].astype(f32)
    ret_p = np.stack([res[2 * b]["ret_p"] for b in range(B)])[None].astype(f32)
    kc_p = np.stack([res[2 * b]["kc_p"].reshape(WK, 16, 64) for b in range(B)])[None].astype(f32)
    vc_p = np.stack([res[2 * b]["vc_p"].reshape(WK, 16, 64) for b in range(B)])[None].astype(f32)
    ka_s = np.concatenate([res[c]["ka_s"].reshape(4, 8, 8, 64) for c in range(ncores)])[None].astype(f32)
    va_s = np.concatenate([res[c]["va_s"].reshape(4, 8, 8, 64) for c in range(ncores)])[None].astype(f32)
    ret_s = np.concatenate([res[c]["ret_s"] for c in range(ncores)])[None].astype(f32)
    kc_s = np.concatenate([res[c]["kc_s"].reshape(4, W_MAX, 16, 64) for c in range(ncores)])[None].astype(f32)
    vc_s = np.concatenate([res[c]["vc_s"].reshape(4, W_MAX, 16, 64) for c in range(ncores)])[None].astype(f32)
    return (y_p, y_s, ka_p, va_p, ret_p, kc_p, vc_p, ka_s, va_s, ret_s, kc_s, vc_s)
```
